# Optimizing a Trainium2 kernel written in Bass

```python
import math
import jax
import jax.numpy as jnp
from jax import lax
import numpy as np

D_MODEL = 4096
BATCH = 16
SEQ = 256
DEPTH = 2
DEC_BATCH = 4
DEC_SEQ = 1024
PAST_LEN = 256

GRID_W = 64
HEAD_DIM = 128
MIX_W = D_MODEL
GROUP_W = MIX_W // 4
A_HEADS = GROUP_W // HEAD_DIM
A_DH = HEAD_DIM // 2
B_HEADS = GROUP_W // HEAD_DIM
B_KV = B_HEADS // 4
C_HEADS = GROUP_W // HEAD_DIM
C_KV = C_HEADS // 4
WINDOW = 128
SSM_P = 64
SSM_INNER = GROUP_W
SSM_HEADS = SSM_INNER // SSM_P
SSM_GROUPS = 2
SSM_N = 128
SSM_CHUNK = 128
D_CONV = 5
CONV_DIM = SSM_INNER + 2 * SSM_GROUPS * SSM_N
D_FF = 4 * D_MODEL
BLOCK_Q = 128
ROPE_BASE = 10000.0
EPS = 1e-6
N_MOD = 6
IN_SPLITS = (A_HEADS * 2 * A_DH, A_HEADS * 2 * A_DH, A_HEADS * 2 * A_DH,
             B_HEADS * HEAD_DIM, B_KV * HEAD_DIM, B_KV * HEAD_DIM,
             C_HEADS * HEAD_DIM, C_KV * HEAD_DIM, C_KV * HEAD_DIM,
             SSM_INNER, CONV_DIM, 2 * SSM_HEADS)
IN_W = sum(IN_SPLITS)
SPLIT_IDX = [int(v) for v in np.cumsum(IN_SPLITS)[:-1]]

kernel_name = 'hybrid_prefix_flow_trunk_step'


def rms_norm(x, g):
    xf = x.astype(jnp.float32)
    y = xf * lax.rsqrt(jnp.mean(xf * xf, axis=-1, keepdims=True) + EPS)
    return (y * g.astype(jnp.float32)).astype(x.dtype)


def axial_rope(x):
    n_tok, d = x.shape[1], x.shape[-1]
    n_rows = n_tok // GRID_W
    row = jnp.repeat(jnp.arange(n_rows), GRID_W, total_repeat_length=n_tok)
    col = jnp.tile(jnp.arange(GRID_W), n_rows)
    da = d // 2
    inv_freq = 1.0 / (ROPE_BASE ** (jnp.arange(0, da, 2, dtype=jnp.float32) / da))
    xf = x.astype(jnp.float32)

    def rotate(xa, pos):
        ang = pos.astype(jnp.float32)[:, None] * inv_freq
        cos = jnp.cos(ang)[None, :, None, :]
        sin = jnp.sin(ang)[None, :, None, :]
        x1, x2 = xa[..., : da // 2], xa[..., da // 2:]
        return jnp.concatenate([x1 * cos - x2 * sin, x2 * cos + x1 * sin], axis=-1)

    out = jnp.concatenate([rotate(xf[..., :da], row), rotate(xf[..., da:], col)], axis=-1)
    return out.astype(x.dtype)


def rope_pairs(u):
    s = u.shape
    return axial_rope(u.reshape(s[0], s[1], s[2] * s[3], s[4])).reshape(s)


def dense_gqa(q, k, v, sink):
    bsz, n_q, hq, d = q.shape
    hkv = k.shape[2]
    g = hq // hkv
    nb = n_q // BLOCK_Q
    qb = jnp.moveaxis(q.reshape(bsz, nb, BLOCK_Q, hkv, g, d), 1, 0)
    scale = d ** -0.5

    def block(qblk):
        s = jnp.einsum('bqhgd,bkhd->bhgqk', qblk, k).astype(jnp.float32) * scale
        if sink is None:
            p = jax.nn.softmax(s, axis=-1)
        else:
            sk = jnp.broadcast_to(sink.astype(jnp.float32).reshape(hkv, g)[None, :, :, None, None], s.shape[:-1] + (1,))
            p = jax.nn.softmax(jnp.concatenate([s, sk], axis=-1), axis=-1)[..., :-1]
        return jnp.einsum('bhgqk,bkhd->bqhgd', p.astype(v.dtype), v)

    o = lax.map(block, qb)
    return jnp.moveaxis(o, 0, 1).reshape(bsz, n_q, hq, d)


def diff_attention(q, k, v, lam):
    bsz, n_q, nh, _, dh = q.shape
    nb = n_q // BLOCK_Q
    qb = jnp.moveaxis(q.reshape(bsz, nb, BLOCK_Q, nh, 2, dh), 1, 0)
    scale = dh ** -0.5

    def block(qblk):
        s = jnp.einsum('bqhcd,bkhcd->bhcqk', qblk, k).astype(jnp.float32) * scale
        p = jax.nn.softmax(s, axis=-1)
        p = p[:, :, 0] - lam * p[:, :, 1]
        return jnp.einsum('bhqk,bkhd->bqhd', p.astype(v.dtype), v)

    o = lax.map(block, qb)
    return jnp.moveaxis(o, 0, 1).reshape(bsz, n_q, nh, v.shape[-1])


def window_attention(q, k, v, k_ctx, v_ctx, sink):
    bsz, n_tok, hq, d = q.shape
    hkv = k.shape[2]
    g = hq // hkv
    nb = n_tok // BLOCK_Q
    n_ctx = k_ctx.shape[1]
    pad = ((0, 0), (BLOCK_Q, BLOCK_Q), (0, 0), (0, 0))
    kp = jnp.pad(k, pad).reshape(bsz, nb + 2, BLOCK_Q, hkv, d)
    vp = jnp.pad(v, pad).reshape(bsz, nb + 2, BLOCK_Q, hkv, d)
    kw = jnp.concatenate([kp[:, :-2], kp[:, 1:-1], kp[:, 2:]], axis=2)
    vw = jnp.concatenate([vp[:, :-2], vp[:, 1:-1], vp[:, 2:]], axis=2)
    qb = q.reshape(bsz, nb, BLOCK_Q, hkv, g, d)
    scale = d ** -0.5
    s_loc = jnp.einsum('bnqhgd,bnkhd->bnhgqk', qb, kw).astype(jnp.float32) * scale
    qpos = jnp.arange(nb)[:, None] * BLOCK_Q + jnp.arange(BLOCK_Q)[None, :]
    kpos = jnp.arange(nb)[:, None] * BLOCK_Q - BLOCK_Q + jnp.arange(3 * BLOCK_Q)[None, :]
    valid = (kpos >= 0) & (kpos < n_tok)
    mask = (jnp.abs(qpos[:, :, None] - kpos[:, None, :]) <= WINDOW) & valid[:, None, :]
    s_loc = jnp.where(mask[None, :, None, None], s_loc, -jnp.inf)
    s_ctx = jnp.einsum('bnqhgd,bkhd->bnhgqk', qb, k_ctx).astype(jnp.float32) * scale
    s_sink = jnp.broadcast_to(sink.astype(jnp.float32).reshape(hkv, g)[None, None, :, :, None, None], s_ctx.shape[:-1] + (1,))
    p = jax.nn.softmax(jnp.concatenate([s_ctx, s_loc, s_sink], axis=-1), axis=-1)
    p_ctx = p[..., :n_ctx].astype(v.dtype)
    p_loc = p[..., n_ctx:n_ctx + 3 * BLOCK_Q].astype(v.dtype)
    o = jnp.einsum('bnhgqk,bkhd->bnqhgd', p_ctx, v_ctx) + jnp.einsum('bnhgqk,bnkhd->bnqhgd', p_loc, vw)
    return o.reshape(bsz, n_tok, hq, d)


def centred_conv(u, w, b):
    out = lax.conv_general_dilated(
        u, w[:, None, :].astype(u.dtype), window_strides=(1,),
        padding=[(D_CONV // 2, D_CONV // 2)], dimension_numbers=('NWC', 'WIO', 'NWC'),
        feature_group_count=u.shape[-1])
    return out + b.astype(u.dtype)


def ssd_scan(x, dt, a, bm, cm, h0):
    bsz, n_tok, nh, hp = x.shape
    ng, ns = bm.shape[2], bm.shape[3]
    hg = nh // ng
    nc = n_tok // SSM_CHUNK
    xc = x.reshape(bsz, nc, SSM_CHUNK, ng, hg, hp)
    dtc = dt.reshape(bsz, nc, SSM_CHUNK, ng, hg)
    bc = bm.reshape(bsz, nc, SSM_CHUNK, ng, ns)
    cc = cm.reshape(bsz, nc, SSM_CHUNK, ng, ns)
    acum = jnp.cumsum(dtc * a.reshape(ng, hg), axis=2)
    lower = jnp.tril(jnp.ones((SSM_CHUNK, SSM_CHUNK), dtype=bool))[:, :, None, None]
    seg = acum[:, :, :, None] - acum[:, :, None, :]
    decay = jnp.exp(jnp.where(lower, seg, -jnp.inf))
    cb = jnp.einsum('bcign,bcjgn->bcijg', cc, bc)
    xdt = xc * dtc[..., None]
    y_diag = jnp.einsum('bcijgh,bcjghp->bcighp', cb[..., None] * decay, xdt)
    to_end = jnp.exp(acum[:, :, -1:] - acum)
    states = jnp.einsum('bcjgn,bcjghp->bcghpn', bc, xdt * to_end[..., None])
    chunk_decay = jnp.exp(acum[:, :, -1])

    def step(h, inp):
        st, dec = inp
        return h * dec[..., None, None] + st, h

    h_last, h_in = lax.scan(step, h0.reshape(bsz, ng, hg, hp, ns),
                            (jnp.moveaxis(states, 1, 0), jnp.moveaxis(chunk_decay, 1, 0)))
    h_in = jnp.moveaxis(h_in, 0, 1)
    y_off = jnp.einsum('bcign,bcghpn->bcighp', cc, h_in) * jnp.exp(acum)[..., None]
    return (y_diag + y_off).reshape(bsz, n_tok, nh, hp), h_last.reshape(bsz, nh, hp, ns)


def flip_seq(u):
    return jnp.flip(u, axis=1)


def bidir_ssd(z, xbc, dt_raw, conv_w, conv_b, dt_bias, a_log, d_skip, g_norm, h0):
    bsz, n_tok, _ = z.shape
    xbc = jax.nn.silu(centred_conv(xbc, conv_w, conv_b)).astype(jnp.float32)
    xs, bm, cm = jnp.split(xbc, [SSM_INNER, SSM_INNER + SSM_GROUPS * SSM_N], axis=-1)
    xs = xs.reshape(bsz, n_tok, SSM_HEADS, SSM_P)
    bm = bm.reshape(bsz, n_tok, SSM_GROUPS, SSM_N)
    cm = cm.reshape(bsz, n_tok, SSM_GROUPS, SSM_N)
    dt = jax.nn.softplus(dt_raw.astype(jnp.float32).reshape(bsz, n_tok, 2, SSM_HEADS) + dt_bias.astype(jnp.float32))
    a = -jnp.exp(a_log.astype(jnp.float32))
    h0 = h0.astype(jnp.float32)
    y_f, h_f = ssd_scan(xs, dt[:, :, 0], a[0], bm, cm, h0[:, 0])
    y_b, h_b = ssd_scan(flip_seq(xs), flip_seq(dt[:, :, 1]), a[1], flip_seq(bm), flip_seq(cm), h0[:, 1])
    d_tot = (d_skip[0] + d_skip[1]).astype(jnp.float32)
    y = y_f + flip_seq(y_b) + d_tot[:, None] * xs
    y = y.reshape(bsz, n_tok, SSM_INNER) * jax.nn.silu(z.astype(jnp.float32))
    yg = y.reshape(bsz, n_tok, SSM_GROUPS, SSM_INNER // SSM_GROUPS)
    yg = yg * lax.rsqrt(jnp.mean(yg * yg, axis=-1, keepdims=True) + EPS)
    y = yg.reshape(bsz, n_tok, SSM_INNER) * g_norm.astype(jnp.float32)
    return y.astype(z.dtype), jnp.stack([h_f, h_b], axis=1).astype(z.dtype)


def trunk_layer(x, cvec, lp, layer_idx, ctx_cache):
    (w_mod, b_mod, g_pre_mix, g_post_mix, g_pre_ffn, g_post_ffn, w_in, w_out,
     lam_q1, lam_k1, lam_q2, lam_k2, g_subln, g_qnorm, g_knorm, sink,
     conv_w, conv_b, dt_bias, a_log, d_skip, g_ssm_norm, w_up, w_down) = lp
    bsz, n_tok, _ = x.shape
    lambda_init = 0.8 - 0.6 * math.exp(-0.3 * layer_idx)
    sh1, sc1, gt1, sh2, sc2, gt2 = jnp.split(jax.nn.silu(cvec) @ w_mod + b_mod, N_MOD, axis=-1)

    h = rms_norm(x, g_pre_mix) * (1.0 + sc1) + sh1
    qa, ka, va, qb, kb, vb, qc, kc, vc, z, xbc, dt_raw = jnp.split(h @ w_in, SPLIT_IDX, axis=-1)
    qa = qa.reshape(bsz, n_tok, A_HEADS, 2, A_DH)
    ka = ka.reshape(bsz, n_tok, A_HEADS, 2, A_DH)
    va = va.reshape(bsz, n_tok, A_HEADS, 2 * A_DH)
    qb = rms_norm(qb.reshape(bsz, n_tok, B_HEADS, HEAD_DIM), g_qnorm)
    kb = rms_norm(kb.reshape(bsz, n_tok, B_KV, HEAD_DIM), g_knorm)
    vb = vb.reshape(bsz, n_tok, B_KV, HEAD_DIM)
    qc = qc.reshape(bsz, n_tok, C_HEADS, HEAD_DIM)
    kc = kc.reshape(bsz, n_tok, C_KV, HEAD_DIM)
    vc = vc.reshape(bsz, n_tok, C_KV, HEAD_DIM)
    lam = (jnp.exp(jnp.sum(lam_q1.astype(jnp.float32) * lam_k1.astype(jnp.float32)))
           - jnp.exp(jnp.sum(lam_q2.astype(jnp.float32) * lam_k2.astype(jnp.float32))) + lambda_init)

    if ctx_cache is None:
        oa = diff_attention(qa, ka, va, lam)
        ob = dense_gqa(qb, kb, vb, None)
        oc = dense_gqa(qc, kc, vc, sink)
        h0 = jnp.zeros((bsz, 2, SSM_HEADS, SSM_P, SSM_N), jnp.float32)
    else:
        ck_a, cv_a, ck_b, cv_b, ck_c, cv_c, h0 = ctx_cache
        n_ctx = ck_a.shape[1]
        ka_all = jnp.concatenate([ck_a.reshape(bsz, n_ctx, A_HEADS, 2, A_DH), rope_pairs(ka)], axis=1)
        oa = diff_attention(rope_pairs(qa), ka_all, jnp.concatenate([cv_a, va], axis=1), lam)
        ob = dense_gqa(axial_rope(qb), jnp.concatenate([ck_b, axial_rope(kb)], axis=1),
                       jnp.concatenate([cv_b, vb], axis=1), None)
        oc = window_attention(axial_rope(qc), axial_rope(kc), vc, ck_c, cv_c, sink)
    oa = rms_norm(oa, g_subln) * (1.0 - lambda_init)
    od, h_end = bidir_ssd(z, xbc, dt_raw, conv_w, conv_b, dt_bias, a_log, d_skip, g_ssm_norm, h0)

    mix = jnp.concatenate([oa.reshape(bsz, n_tok, GROUP_W), ob.reshape(bsz, n_tok, GROUP_W),
                           oc.reshape(bsz, n_tok, GROUP_W), od], axis=-1) @ w_out
    x = x + gt1 * rms_norm(mix, g_post_mix)

    h2 = rms_norm(x, g_pre_ffn) * (1.0 + sc2) + sh2
    u = jax.nn.relu(h2 @ w_up)
    x = x + gt2 * rms_norm((u * u) @ w_down, g_post_ffn)

    if ctx_cache is None:
        ctx_out = (ka.reshape(bsz, n_tok, A_HEADS, 2 * A_DH), va, kb, vb, kc, vc, h_end)
    else:
        ctx_out = None
    return x, ctx_out


def setup_inputs(seed: int = 0) -> dict:
    key = jax.random.key(seed)
    keys = jax.random.split(key, 40)
    counter = iter(range(40))
    f32 = jnp.float32

    def nrm(shape, scale):
        return jax.random.normal(keys[next(counter)], shape, f32) * scale

    def gain(shape):
        return 1.0 + nrm(shape, 0.02)

    def unif(shape, lo, hi):
        return jax.random.uniform(keys[next(counter)], shape, f32, lo, hi)

    dt0 = jnp.exp(unif((DEPTH, 2, SSM_HEADS), math.log(1e-3), math.log(1e-1)))
    dt_bias = dt0 + jnp.log(-jnp.expm1(-dt0))
    a_log = jnp.log(unif((DEPTH, 2, SSM_HEADS), 1.0, 16.0))
    return {
        'x_prompt': nrm((BATCH, SEQ, D_MODEL), 1.0),
        'x_sample': nrm((DEC_BATCH, DEC_SEQ, D_MODEL), 1.0),
        'c': nrm((DEC_BATCH, D_MODEL), 1.0),
        'cache_a_k': nrm((DEC_BATCH, DEPTH, PAST_LEN, A_HEADS, 2 * A_DH), 1.0),
        'cache_a_v': nrm((DEC_BATCH, DEPTH, PAST_LEN, A_HEADS, 2 * A_DH), 1.0),
        'cache_b_k': nrm((DEC_BATCH, DEPTH, PAST_LEN, B_KV, HEAD_DIM), 1.0),
        'cache_b_v': nrm((DEC_BATCH, DEPTH, PAST_LEN, B_KV, HEAD_DIM), 1.0),
        'cache_c_k': nrm((DEC_BATCH, DEPTH, PAST_LEN, C_KV, HEAD_DIM), 1.0),
        'cache_c_v': nrm((DEC_BATCH, DEPTH, PAST_LEN, C_KV, HEAD_DIM), 1.0),
        'state_ssm': nrm((DEC_BATCH, DEPTH, 2, SSM_HEADS, SSM_P, SSM_N), 0.1),
        'c_ctx': nrm((D_MODEL,), 1.0),
        'w_mod': nrm((DEPTH, D_MODEL, N_MOD * D_MODEL), 0.5 * D_MODEL ** -0.5),
        'b_mod': nrm((DEPTH, N_MOD * D_MODEL), 0.02),
        'g_pre_mix': gain((DEPTH, D_MODEL)),
        'g_post_mix': gain((DEPTH, D_MODEL)),
        'g_pre_ffn': gain((DEPTH, D_MODEL)),
        'g_post_ffn': gain((DEPTH, D_MODEL)),
        'w_in': nrm((DEPTH, D_MODEL, IN_W), D_MODEL ** -0.5),
        'w_out': nrm((DEPTH, MIX_W, D_MODEL), MIX_W ** -0.5),
        'lam_q1': nrm((DEPTH, A_DH), 0.1),
        'lam_k1': nrm((DEPTH, A_DH), 0.1),
        'lam_q2': nrm((DEPTH, A_DH), 0.1),
        'lam_k2': nrm((DEPTH, A_DH), 0.1),
        'g_subln': gain((DEPTH, 2 * A_DH)),
        'g_qnorm': gain((DEPTH, HEAD_DIM)),
        'g_knorm': gain((DEPTH, HEAD_DIM)),
        'sink': nrm((DEPTH, C_HEADS), 0.5),
        'conv_w': nrm((DEPTH, D_CONV, CONV_DIM), D_CONV ** -0.5),
        'conv_b': nrm((DEPTH, CONV_DIM), 0.02),
        'dt_bias': dt_bias,
        'a_log': a_log,
        'd_skip': 1.0 + nrm((DEPTH, 2, SSM_HEADS), 0.1),
        'g_ssm_norm': gain((DEPTH, SSM_INNER)),
        'w_up': nrm((DEPTH, D_MODEL, D_FF), D_MODEL ** -0.5),
        'w_down': nrm((DEPTH, D_FF, D_MODEL), D_FF ** -0.5),
    }


def reference(x_prompt, x_sample, c, cache_a_k, cache_a_v, cache_b_k, cache_b_v, cache_c_k, cache_c_v,
              state_ssm, c_ctx, w_mod, b_mod, g_pre_mix, g_post_mix, g_pre_ffn, g_post_ffn, w_in, w_out,
              lam_q1, lam_k1, lam_q2, lam_k2, g_subln, g_qnorm, g_knorm, sink, conv_w, conv_b,
              dt_bias, a_log, d_skip, g_ssm_norm, w_up, w_down):
    layer_weights = (w_mod, b_mod, g_pre_mix, g_post_mix, g_pre_ffn, g_post_ffn, w_in, w_out,
                     lam_q1, lam_k1, lam_q2, lam_k2, g_subln, g_qnorm, g_knorm, sink,
                     conv_w, conv_b, dt_bias, a_log, d_skip, g_ssm_norm, w_up, w_down)
    ctx_vec = c_ctx[None, None, :]
    lat_vec = c[:, None, :]
    y_prompt = x_prompt
    y_sample = x_sample
    ctx_layers = []
    for layer in range(DEPTH):
        lp = [w[layer] for w in layer_weights]
        y_prompt, ctx_out = trunk_layer(y_prompt, ctx_vec, lp, layer, None)
        ctx_layers.append(ctx_out)
        cached = (cache_a_k[:, layer], cache_a_v[:, layer], cache_b_k[:, layer], cache_b_v[:, layer],
                  cache_c_k[:, layer], cache_c_v[:, layer], state_ssm[:, layer])
        y_sample, _ = trunk_layer(y_sample, lat_vec, lp, layer, cached)
    new_a_k, new_a_v, new_b_k, new_b_v, new_c_k, new_c_v, new_ssm = [
        jnp.stack([t[i] for t in ctx_layers], axis=1) for i in range(7)]
    return (y_prompt, y_sample, new_a_k, new_a_v, new_b_k, new_b_v, new_c_k, new_c_v, new_ssm)
```

```python
import math
import numpy as np
import concourse.bass as bass
import concourse.mybir as mybir
from concourse.bass_utils import run_bass_kernel_spmd

F32 = mybir.dt.float32
BF16 = mybir.dt.bfloat16
AF = mybir.ActivationFunctionType
ALU = mybir.AluOpType

T = 1024
D = 4096
NJ = 32
NT = 8
DFF = 16384
INW = 8736
EPS = 1e-6
NEG = -30000.0
OQA, OKA, OVA = 0, 1024, 2048
OQB, OKB, OVB = 3072, 4096, 4352
OQC, OKC, OVC = 4608, 5632, 5888
OZ, OXBC, ODT = 6144, 7168, 8704


class Chan:
    def __init__(self, sem):
        self.sem = sem
        self.count = 0
        self.last = None


class Op:
    __slots__ = ("eng", "fn", "chan", "deps", "signal", "sig_idx", "dma_count")

    def __init__(self, eng, fn, chan):
        self.eng = eng
        self.fn = fn
        self.chan = chan
        self.deps = ()
        self.signal = False
        self.sig_idx = 0
        self.dma_count = 0


class Sched:
    ENG = {"pe": "tensor", "act": "scalar", "dve": "vector", "pool": "gpsimd", "sp": "sync"}

    def __init__(self, nc):
        self.nc = nc
        self.ops = []
        self.last_w = {}
        self.readers = {}
        self.esem = {e: nc.alloc_semaphore(name="sem_" + e) for e in ("pe", "act", "dve", "pool", "sp")}
        self.chans = []
        self.last_on = {}

    def chan(self):
        c = Chan(self.nc.alloc_semaphore(name="dch%d" % len(self.chans)))
        self.chans.append(c)
        return c

    def add(self, eng, fn, reads=(), writes=(), chan=None, extra=()):
        op = Op(eng, fn, chan)
        pr = [k for k in reads if isinstance(k, tuple) and k[0] == "ps"]
        if pr:
            reads = [k for k in reads if k not in pr]
            writes = list(writes) + pr
        deps = {}
        for k in reads:
            w = self.last_w.get(k)
            if w is not None:
                deps[id(w)] = w
        for k in writes:
            w = self.last_w.get(k)
            if w is not None:
                deps[id(w)] = w
            rd = self.readers.get(k)
            if rd:
                for r in rd.values():
                    deps[id(r)] = r
        for d in extra:
            deps[id(d)] = d
        dl = []
        for d in deps.values():
            if d.chan is not None or d.eng != eng or eng in ("act", "dve"):
                d.signal = True
                dl.append(d)
        op.deps = dl
        rk = id(chan) if chan is not None else eng
        for k in reads:
            self.readers.setdefault(k, {})[rk] = op
        for k in writes:
            self.last_w[k] = op
            self.readers[k] = {}
        self.ops.append(op)
        if fn is not None:
            if chan is not None:
                chan.last = op
            else:
                self.last_on[eng] = op
        return op

    def barrier(self):
        lasts = [o for o in self.last_on.values()] + [c.last for c in self.chans if c.last is not None]
        for e in ("pe", "act", "dve", "pool", "sp"):
            self.add(e, None, extra=lasts)
        self.last_w = {}
        self.readers = {}

    def emit(self):
        nc = self.nc
        sig = {e: 0 for e in self.esem}
        waited = {}
        nwait = 0
        for op in self.ops:
            e = getattr(nc, self.ENG[op.eng])
            for d in op.deps:
                if d.chan is not None:
                    sem, val = d.chan.sem, d.dma_count
                else:
                    sem, val = self.esem[d.eng], d.sig_idx
                assert val > 0
                key = (op.eng, sem.num)
                if waited.get(key, 0) < val:
                    e.wait_ge(sem, val)
                    waited[key] = val
                    nwait += 1
            if op.fn is None:
                continue
            inst = op.fn()
            if op.chan is not None:
                op.chan.count += 16
                op.dma_count = op.chan.count
                inst.then_inc(op.chan.sem, 16)
            elif op.signal:
                sig[op.eng] += 1
                op.sig_idx = sig[op.eng]
                inst.then_inc(self.esem[op.eng], 1)
        return dict(n_ops=len(self.ops), n_wait=nwait, sig=sig)


def seq(fns):
    def f():
        r = None
        for g in fns:
            r = g()
        return r
    return f


def cst_layout():
    lay = {}
    off = [0]

    def put(name, n):
        lay[name] = (off[0], n)
        off[0] += n
    put("ident", 128); put("triU", 128); put("triL", 128); put("sL", 128); put("sU", 128); put("ones", 128)
    put("RA", 128); put("RBC", 128)
    put("flag", 1)
    put("maskAB", 40); put("maskC", 1)
    for l in range(2):
        put("bmodT%d" % l, 192)
        for g in ("gpre", "gpost", "gpref", "gpostf"):
            put("%sT%d" % (g, l), 32)
        put("gsubT%d" % l, 1); put("gqT%d" % l, 1); put("gkT%d" % l, 1)
        put("gk_bc%d" % l, 128)
        put("sink_bc%d" % l, 8)
        put("lq1_%d" % l, 64); put("lk1_%d" % l, 64); put("lq2_%d" % l, 64); put("lk2_%d" % l, 64)
        put("convw%d" % l, 60); put("convb%d" % l, 12)
        put("dtb%d" % l, 32); put("alog%d" % l, 32); put("dsk%d" % l, 32)
    return lay, off[0]


CL, NCST = cst_layout()


def build(stop_after=None, dbg=(), lite=(), nl=2):
    nc = bass.Bass("TRN2", target_bir_lowering=False)
    S = Sched(nc)

    used_in = []

    def din(name, shape, dt=F32):
        used_in.append(name)
        if name in lite:
            shape = [1] * len(shape)
        return nc.dram_tensor(name, list(shape), dt, kind="ExternalInput").ap()

    def dout(name, shape, dt=F32):
        return nc.dram_tensor(name, list(shape), dt, kind="ExternalOutput").ap()

    def dscr(name, shape, dt=F32):
        return nc.dram_tensor(name, list(shape), dt, kind="Internal").ap()

    x_in = din("x_in", [T, D]); cT_d = din("cT", [128, 32]); cst_d = din("cst", [128, NCST])
    tabs_d = din("tabs", [4, 128, T]); cmask_d = din("cmask", [128, 2048]); gssm_d = din("gssm", [2, 128, 1024])
    w_mod = din("w_mod", [nl, D, 6 * D]); w_in = din("w_in", [nl, D, INW]); w_out = din("w_out", [nl, D, D])
    w_up = din("w_up", [nl, D, DFF]); w_down = din("w_down", [nl, DFF, D])
    bmod_d = din("b_mod", [2, 6 * D]); gpost_d = din("g_post_mix", [2, D]); gpostf_d = din("g_post_ffn", [2, D])
    cak = din("cak", [2, 256, 1024]); cav = din("cav", [2, 256, 1024])
    cbk = din("cbk", [2, 256, 256]); cbv = din("cbv", [2, 256, 256])
    cck = din("cck", [2, 256, 256]); ccv = din("ccv", [2, 256, 256])
    h0_d = din("h0", [2, 2, 1024, 128])
    y_d = dout("y", [T, D])
    nak = dout("nak", [2, T, 1024]); nav = dout("nav", [2, T, 1024])
    nbk = dout("nbk", [2, T, 256]); nbv = dout("nbv", [2, T, 256])
    nck = dout("nck", [2, T, 256]); ncv = dout("ncv", [2, T, 256])
    nssm = dout("nssm", [2, 4, 2, 1024, 128])
    ymix = dscr("ymix", [T, D]); xmid = dscr("xmid", [T, D]); xl1 = dscr("xl1", [T, D])
    uT_h = dscr("uT_h", [DFF, T], BF16); mixT_h = dscr("mixT_h", [D, T], BF16); modraw = dscr("modraw", [2, 6 * D])
    dbg_out = {}

    def A(name, shape, dt):
        return nc.alloc_sbuf_tensor("sb_" + name, shape, dt)
    bufA = A("bufA", [128, NJ, T], BF16)
    bufB = A("bufB", [128, 32768], BF16)
    NRING = 3
    ring = [A("ring%d" % i, [128, NJ, 256], BF16) for i in range(NRING)]
    cst = A("cst", [128, NCST], F32)
    modT = A("modT", [128, 192], F32)
    gmod1T = A("gmod1T", [128, 32], F32); gmod2T = A("gmod2T", [128, 32], F32)
    sT = A("sT", [128, 32], BF16)
    small = A("small", [128, 64], F32)
    identb = A("identb", [128, 128], BF16); onesb = A("onesb", [128, 128], BF16)
    ps = [nc.alloc_psum_tensor("ps%d" % i, [128, 512], F32) for i in range(8)]

    def C(name, a=None, b=None):
        o, n = CL[name]
        if a is None:
            return cst[:, o:o + n]
        return cst[:, o + a:o + b]

    class Arena:
        def __init__(self):
            self.off = 0
            self.n = 0

        def reset(self):
            self.off = 0

        def alloc(self, shape, dt):
            n = 1
            for s in shape[1:]:
                n *= s
            nb = n * (4 if dt == F32 else 2)
            nb = (nb + 31) // 32 * 32
            assert self.off + nb <= 65536, ("arena overflow", self.off, nb)
            v = bufB[:, self.off // 2:(self.off + nb) // 2]
            self.off += nb
            if dt == F32:
                v = v.bitcast(F32)
            v = v[:, 0:n]
            if len(shape) == 3:
                v = v.rearrange("p (a b) -> p a b", b=shape[2])
            self.n += 1
            return v, ("ar", self.n)

    AR = Arena()

    pe_n = [0]
    marks = []

    def pmark(name):
        marks.append((name, pe_n[0]))

    def mm(out, lhsT, rhs, start=True, stop=True):
        pe_n[0] += 1
        return lambda: nc.tensor.matmul(out, lhsT=lhsT, rhs=rhs, start=start, stop=stop)

    def tr(out, in_, ident):
        pe_n[0] += 1
        return lambda: nc.tensor.transpose(out, in_, ident)

    def act(out, in_, func, bias=None, scale=None, accum=None):
        kw = {}
        if bias is not None:
            kw["bias"] = bias
        if scale is not None:
            kw["scale"] = scale
        if accum is not None:
            kw["accum_out"] = accum
        return lambda: nc.scalar.activation(out=out, in_=in_, func=func, **kw)

    def vtt(out, in0, in1, op):
        return lambda: nc.vector.tensor_tensor(out=out, in0=in0, in1=in1, op=op)

    def vts(out, in0, s1, s2=None, op0=ALU.mult, op1=None):
        if op1 is None:
            return lambda: nc.vector.tensor_scalar(out=out, in0=in0, scalar1=s1, scalar2=None, op0=op0)
        return lambda: nc.vector.tensor_scalar(out=out, in0=in0, scalar1=s1, scalar2=s2, op0=op0, op1=op1)

    def vstt(out, in0, scalar, in1, op0, op1):
        return lambda: nc.vector.scalar_tensor_tensor(out=out, in0=in0, scalar=scalar, in1=in1, op0=op0, op1=op1)

    def vcp(out, in_):
        return lambda: nc.vector.tensor_copy(out=out, in_=in_)

    def vrec(out, in_):
        return lambda: nc.vector.reciprocal(out=out, in_=in_)

    def vset(out, v):
        return lambda: nc.vector.memset(out, v)

    def dma(out, in_, q="sp", nonc=False):
        e = nc.sync if q == "sp" else nc.gpsimd
        if nonc:
            return lambda: e.dma_start(out=out, in_=in_, allow_slow_non_contiguous=True)
        return lambda: e.dma_start(out=out, in_=in_)

    ch_misc = S.chan()
    ch_out = [S.chan() for _ in range(4)]
    ch_ring = [S.chan() for _ in range(NRING)]
    ch_ld = [S.chan() for _ in range(6)]
    out_keys = []
    cnt = {"ring": 0, "out": 0, "ld": 0, "ps": 0}

    def wload(src, ncols, K=D):
        s = cnt["ring"] % NRING
        cnt["ring"] += 1
        S.add("pool", dma(ring[s][:, :, 0:ncols], src.rearrange("(j p) n -> p j n", p=128), q="pool"),
              writes=[("ring", s)], chan=ch_ring[s])
        return s

    def store(dst, src_ap, rkeys):
        c = ch_out[cnt["out"] % 4]
        cnt["out"] += 1
        k = ("hbm", cnt["out"])
        S.add("sp", dma(dst, src_ap), reads=rkeys, writes=[k], chan=c)
        return k

    def load(dst_ap, src, wkeys, rkeys=(), nonc=False):
        c = ch_ld[cnt["ld"] % 6]
        cnt["ld"] += 1
        S.add("sp", dma(dst_ap, src, nonc=nonc), reads=rkeys, writes=wkeys, chan=c)

    def bank():
        b = cnt["ps"] % 8
        cnt["ps"] += 1
        return b

    def dump(name, ap, rkeys, shape):
        if name in dbg:
            d = dout("dbg_" + name, shape, ap.dtype)
            dbg_out[name] = d
            out_keys.append(store(d, ap, rkeys))

    S.add("sp", dma(cst[:], cst_d), writes=["cst"], chan=ch_misc)
    S.add("dve", vcp(identb[:], C("ident")), reads=["cst"], writes=["identb"])
    S.add("dve", vcp(onesb[:], C("ones")), reads=["cst"], writes=["onesb"])
    S.barrier()

    def finish():
        S.barrier()
        st = S.emit()
        st["used_in"] = used_in
        st["marks"] = marks
        return nc, st, dbg_out

    def rstd_from_ss(ss, n, rk, wk):
        S.add("dve", vts(ss, ss, 1.0 / n, EPS, ALU.mult, ALU.add), reads=[rk], writes=[rk])
        S.add("act", act(ss, ss, AF.Sqrt), reads=[rk], writes=[rk])
        S.add("dve", vrec(ss, ss), reads=[rk], writes=[wk])

    def phase_mod(l):
        pmark('phase_mod')
        AR.reset()
        cTt, kc = AR.alloc([128, 32], F32)
        rowb = []
        for i in range(2):
            rowb.append(AR.alloc([128, 2048], F32))
        load(cTt, cT_d, [kc])
        S.add("act", act(sT[:], cTt, AF.Silu), reads=[kc], writes=["sT"])
        hk = []
        if "modin" in dbg:
            mi = din("modraw_in", [2, 6 * D])
            hk.append(store(modraw[l:l + 1, :], mi[l:l + 1, :], []))
        for ng in range(48 if "modin" not in dbg else 0):
            b = bank()
            for kh in range(2):
                s = cnt["ring"] % NRING
                cnt["ring"] += 1
                wv = ring[s][:].rearrange("p j n -> p (j n)").rearrange("p (a b) -> p a b", b=512)
                S.add("pool", dma(wv, w_mod[l][kh * 2048:(kh + 1) * 2048, ng * 512:(ng + 1) * 512].rearrange("(j p) n -> p j n", p=128), q="pool"),
                      writes=[("ring", s)], chan=ch_ring[s])
                S.add("pe", seq([mm(ps[b][0:1, :], sT[:, kh * 16 + j:kh * 16 + j + 1], wv[:, j, :], kh == 0 and j == 0, kh == 1 and j == 15) for j in range(16)]),
                      reads=["sT", ("ring", s)], writes=[("ps", b)])
            rb, rk = rowb[(ng // 4) % 2]
            c0 = (ng % 4) * 512
            S.add("act", act(rb[0:1, c0:c0 + 512], ps[b][0:1, :], AF.Copy), reads=[("ps", b)], writes=[rk])
            if ng % 4 == 3:
                g0 = (ng // 4) * 2048
                hk.append(store(modraw[l:l + 1, g0:g0 + 2048], rb[0:1, :], [rk]))
        stg, stgk = AR.alloc([128, 2, 128], F32)
        load(stg[0:96, :, :], modraw[l].rearrange("(a r p) -> r a p", a=2, p=128), [stgk], rkeys=hk)
        b = bank()
        S.add("pe", seq([tr(ps[b][:, a * 96:(a + 1) * 96], stg[0:96, a, :], C("ident")[0:96, 0:96]) for a in range(2)]),
              reads=[stgk], writes=[("ps", b)])
        S.add("act", act(modT[:], ps[b][:, 0:192], AF.Copy), reads=[("ps", b)], writes=[("modT", 0)])
        mk = [("modT", 0)]
        S.add("dve", vtt(modT[:], modT[:], C("bmodT%d" % l), ALU.add), reads=mk, writes=["modTf"])
        S.add("dve", vstt(gmod1T[:], modT[:, 32:64], 1.0, C("gpreT%d" % l), ALU.add, ALU.mult), reads=["modTf"], writes=["gmod1T"])
        S.add("dve", vstt(gmod2T[:], modT[:, 128:160], 1.0, C("gprefT%d" % l), ALU.add, ALU.mult), reads=["modTf"], writes=["gmod2T"])
        S.barrier()

    def norm_to_hT(xn, xk, tt, gT, shT):
        for g4 in range(8):
            b = bank()
            S.add("pe", seq([tr(ps[b][:, i * 128:(i + 1) * 128], xn[:, (g4 * 4 + i) * 128:(g4 * 4 + i + 1) * 128], C("ident")) for i in range(4)]),
                  reads=[xk], writes=[("ps", b)])
            for i in range(4):
                j = g4 * 4 + i
                o = bufA[:, j, tt * 128:(tt + 1) * 128]
                if False:
                    S.add("act", act(o, ps[b][:, i * 128:(i + 1) * 128], AF.Identity, bias=shT[:, j:j + 1], scale=gT[:, j:j + 1]),
                          reads=[("ps", b)], writes=[("hT", tt)])
                else:
                    S.add("dve", vts(o, ps[b][:, i * 128:(i + 1) * 128], gT[:, j:j + 1], shT[:, j:j + 1], ALU.mult, ALU.add),
                          reads=[("ps", b)], writes=[("hT", tt)])

    def sumsq(xt, xk, junk, jk, ss, sk):
        S.add("dve", vset(ss, 0.0), writes=[sk])
        S.add("act", act(junk, xt, AF.Square, accum=ss), reads=[xk, sk], writes=[jk, sk])

    def phase_n1(l, x_src):
        pmark('phase_n1')
        AR.reset()
        xts = [AR.alloc([128, D], F32) for _ in range(2)]
        junk, jk = AR.alloc([128, D], BF16)
        for tt in range(NT):
            xt, xk = xts[tt % 2]
            load(xt, x_src[tt * 128:(tt + 1) * 128, :], [xk])
            ss = small[:, tt:tt + 1]
            sk = ("ss", tt)
            sumsq(xt, xk, junk, jk, ss, sk)
            rstd_from_ss(ss, D, sk, sk)
            if "n1_nonorm" in dbg:
                continue
            S.add("dve", vts(xt, xt, ss), reads=[xk, sk], writes=[xk])
            if "n1_notr" in dbg:
                continue
            norm_to_hT(xt, xk, tt, gmod1T, modT[:, 0:32])
        S.barrier()

    def proj_fm(s, c0, b0, b1, extra_reads=()):
        for th, b in ((0, b0), (1, b1)):
            S.add("pe", seq([mm(ps[b][:, :], ring[s][:, j, c0:c0 + 128], bufA[:, j, th * 512:(th + 1) * 512], j == 0, j == NJ - 1) for j in range(NJ)]),
                  reads=[("ring", s), "hTall"] + list(extra_reads), writes=[("ps", b)])

    def proj_tm(s, tt, n, b, col0=0):
        S.add("pe", seq([mm(ps[b][:, col0:col0 + n], bufA[:, j, tt * 128:(tt + 1) * 128], ring[s][:, j, 0:n], j == 0, j == NJ - 1) for j in range(NJ)]),
              reads=[("ring", s), "hTall"], writes=[("ps", b)])

    def rope(src32, sk, R, cosT, sinT, tk, out_bf, ok, t1, t1k, b0, b1):
        for th, b in ((0, b0), (1, b1)):
            sl = slice(th * 512, (th + 1) * 512)
            S.add("pe", mm(ps[b][:, :], R, src32[:, sl]), reads=[sk], writes=[("ps", b)])
            S.add("dve", vtt(t1[:, sl], ps[b][:, :], sinT[:, sl], ALU.mult), reads=[("ps", b), tk], writes=[t1k])
        S.add("dve", vtt(src32, src32, cosT, ALU.mult), reads=[sk, tk], writes=[sk])
        if isinstance(out_bf, list):
            for o_, prt in out_bf:
                S.add("dve", vtt(o_[prt, :], src32[prt, :], t1[prt, :], ALU.add), reads=[sk, t1k], writes=[ok])
        else:
            S.add("dve", vtt(out_bf, src32, t1, ALU.add), reads=[sk, t1k], writes=[ok])

    def cache_kT(src, dstT, dk, stg, stgk, wd=128):
        load(stg, src.rearrange("(a p) d -> p a d", p=128), [stgk])
        b = bank()
        S.add("pe", seq([tr(ps[b][:, a * 128:(a + 1) * 128], stg[:, a, :], C("ident")) for a in range(2)]),
              reads=[stgk], writes=[("ps", b)])
        S.add("act", act(dstT, ps[b][:, 0:256], AF.Copy), reads=[("ps", b)], writes=[dk])

    def lam_compute(l, lam, lamk):
        li = 0.8 - 0.6 * math.exp(-0.3 * l)
        t, tk = AR.alloc([128, 64], F32)
        e1 = small[:, 16:17]; e2 = small[:, 17:18]
        S.add("dve", vtt(t, C("lq1_%d" % l), C("lk1_%d" % l), ALU.mult), writes=[tk])
        S.add("dve", lambda: nc.vector.reduce_sum(out=e1, in_=t, axis=mybir.AxisListType.X), reads=[tk], writes=["e1"])
        S.add("dve", vtt(t, C("lq2_%d" % l), C("lk2_%d" % l), ALU.mult), reads=["e1"], writes=[tk])
        S.add("dve", lambda: nc.vector.reduce_sum(out=e2, in_=t, axis=mybir.AxisListType.X), reads=[tk], writes=["e2"])
        S.add("act", act(e1, e1, AF.Exp), reads=["e1"], writes=["e1"])
        S.add("act", act(e2, e2, AF.Exp), reads=["e2"], writes=["e2"])
        S.add("dve", vtt(lam, e1, e2, ALU.subtract), reads=["e1", "e2"], writes=[lamk])
        S.add("dve", vts(lam, lam, li, None, ALU.add), reads=[lamk], writes=[lamk])
        return li

    def mix_store(chunk, src_bf, rk):
        out_keys.append(store(mixT_h[chunk * 128:(chunk + 1) * 128, :], src_bf, [rk]))

    def attn_core(nmaps, kparts, qT, qk, kT, kk, vb, vk, vcol, nkt, scale, pT, oacc, dacc, sbanks, maskcol):
        it = 0
        for qh in range(2):
            for kt in range(nkt):
                for m in range(nmaps):
                    pr = kparts[m]
                    sb = sbanks[it % len(sbanks)]
                    p, pk = pT[it % len(pT)]
                    it += 1
                    S.add("pe", mm(ps[sb][:, :], kT[pr, kt * 128:(kt + 1) * 128], qT[pr, qh * 512:(qh + 1) * 512]),
                          reads=[qk, kk], writes=[("ps", sb)])
                    for qb in range(2):
                        mc = maskcol(kt, qh * 2 + qb)
                        S.add("act", act(p[:, qb * 256:(qb + 1) * 256], ps[sb][:, qb * 256:(qb + 1) * 256], AF.Exp, bias=mc, scale=scale),
                              reads=[("ps", sb)], writes=[pk])
                    ob = oacc[m][qh]; db = dacc[m][qh]
                    S.add("pe", mm(ps[ob][:, :], vb[:, kt, vcol], p, kt == 0, kt == nkt - 1), reads=[pk, vk], writes=[("ps", ob)])
                    S.add("pe", mm(ps[db][:, :], onesb[:], p, kt == 0, kt == nkt - 1), reads=[pk, "onesb"], writes=[("ps", db)])

    def phase_mixA(l):
        pmark('phase_mixA')
        AR.reset()
        cosT, ck = AR.alloc([128, T], F32); sinT, _ = AR.alloc([128, T], F32)
        load(cosT, tabs_d[0], [ck]); load(sinT, tabs_d[1], [ck])
        lam, lamk = small[:, 18:19], "lam"
        li = lam_compute(l, lam, lamk)
        nlam, nlk = small[:, 19:20], "nlam"
        S.add("dve", vts(nlam, lam, -1.0), reads=[lamk], writes=[nlk])
        q32, q32k = AR.alloc([128, T], F32); k32, k32k = AR.alloc([128, T], F32)
        t1, t1k = AR.alloc([128, T], F32)
        qTbs = [AR.alloc([128, 2, T], BF16) for _ in range(2)]
        kTas = [AR.alloc([128, 1280], BF16) for _ in range(2)]
        for qTb, qTk in qTbs:
            S.add("dve", vset(qTb[64:128, 0, :], 0.0), writes=[qTk])
            S.add("dve", vset(qTb[0:64, 1, :], 0.0), writes=[qTk])
        vbs = [AR.alloc([128, 10, 256], BF16) for _ in range(2)]
        vst = [AR.alloc([128, 256], F32) for _ in range(2)]
        kst = [AR.alloc([128, 256], F32) for _ in range(2)]
        cks, cksk = AR.alloc([128, 2, 128], F32); cvs, cvsk = AR.alloc([128, 2, 256], F32)
        pT = [AR.alloc([128, 512], BF16) for _ in range(4)]
        e1, e1k = AR.alloc([128, 512], F32); e2, e2k = AR.alloc([128, 512], F32); e3, e3k = AR.alloc([128, 512], F32)
        ob, obk = AR.alloc([128, T], BF16)
        slots = {}

        def pair_prep(hp):
            sq = wload(w_in[l][:, OQA + hp * 256:OQA + (hp + 1) * 256], 256)
            sk_ = wload(w_in[l][:, OKA + hp * 256:OKA + (hp + 1) * 256], 256)
            sv = wload(w_in[l][:, OVA + hp * 256:OVA + (hp + 1) * 256], 256)
            slots[hp] = (sq, sk_)
            vb, vk = vbs[hp % 2]
            for tt in range(NT):
                b = bank()
                proj_tm(sk_, tt, 256, b)
                st, stk = kst[tt % 2]
                S.add("act", act(st, ps[b][:, 0:256], AF.Copy), reads=[("ps", b)], writes=[stk])
                out_keys.append(store(nak[l, tt * 128:(tt + 1) * 128, hp * 256:(hp + 1) * 256], st, [stk]))
                b = bank()
                proj_tm(sv, tt, 256, b)
                st, stk = vst[tt % 2]
                S.add("act", act(st, ps[b][:, 0:256], AF.Copy), reads=[("ps", b)], writes=[stk])
                S.add("dve", vcp(vb[:, 2 + tt, :], st), reads=[stk], writes=[vk])
                out_keys.append(store(nav[l, tt * 128:(tt + 1) * 128, hp * 256:(hp + 1) * 256], st, [stk]))
            load(cvs, cav[l][:, hp * 256:(hp + 1) * 256].rearrange("(a p) d -> p a d", p=128), [cvsk])
            S.add("dve", vcp(vb[:, 0:2, :], cvs), reads=[cvsk], writes=[vk])

        def prep(h):
            hp, hh = h // 2, h % 2
            sq, sk_ = slots[hp]
            c0 = hh * 128
            qTb, qTk = qTbs[h % 2]
            kTa, kTk = kTas[h % 2]
            proj_fm(sq, c0, 0, 1)
            S.add("act", act(q32[:, 0:512], ps[0][:, :], AF.Copy), reads=[("ps", 0)], writes=[q32k])
            S.add("act", act(q32[:, 512:1024], ps[1][:, :], AF.Copy), reads=[("ps", 1)], writes=[q32k])
            rope(q32, q32k, C("RA"), cosT, sinT, ck, [(qTb[:, 0, :], slice(0, 64)), (qTb[:, 1, :], slice(64, 128))], qTk, t1, t1k, 2, 3)
            proj_fm(sk_, c0, 0, 1)
            S.add("act", act(k32[:, 0:512], ps[0][:, :], AF.Copy), reads=[("ps", 0)], writes=[k32k])
            S.add("act", act(k32[:, 512:1024], ps[1][:, :], AF.Copy), reads=[("ps", 1)], writes=[k32k])
            rope(k32, k32k, C("RA"), cosT, sinT, ck, kTa[:, 256:1280], kTk, t1, t1k, 2, 3)
            cache_kT(cak[l][:, h * 128:(h + 1) * 128], kTa[:, 0:256], kTk, cks, cksk)

        pair_prep(0)
        prep(0)
        for h in range(8):
            if h + 1 < 8:
                if (h + 1) % 2 == 0:
                    pair_prep((h + 1) // 2)
                prep(h + 1)
            qTb, qTk = qTbs[h % 2]
            kTa, kTk = kTas[h % 2]
            vb, vk = vbs[(h // 2) % 2]
            _attnA(l, h, (h % 2) * 128, qTb, qTk, kTa, kTk, vb, vk, pT, e1, e1k, e2, e2k, e3, e3k, ob, obk, nlam, nlk, li)
        S.barrier()

    def _attnA(l, h, c0, qTb, qTk, kTa, kTk, vb, vk, pT, e1, e1k, e2, e2k, e3, e3k, ob, obk, nlam, nlk, li):
        LA = 2
        it0 = [0]
        for qh in range(2):
            qs = slice(qh * 512, (qh + 1) * 512)
            items = [(kt, m) for kt in range(10) for m in range(2)]
            base = it0[0]
            it0[0] += len(items)

            def score(i, qh=qh, qs=qs, base=base):
                kt, m = items[i]
                sb = 4 + (base + i) % 4
                p, pk = pT[(base + i) % 4]
                S.add("pe", mm(ps[sb][:, :], kTa[:, kt * 128:(kt + 1) * 128], qTb[:, m, qs]), reads=[qTk, kTk], writes=[("ps", sb)])
                for qb in range(2):
                    o = kt * 4 + qh * 2 + qb
                    S.add("act", act(p[:, qb * 256:(qb + 1) * 256], ps[sb][:, qb * 256:(qb + 1) * 256], AF.Exp, bias=C("maskAB", o, o + 1), scale=0.125),
                          reads=[("ps", sb)], writes=[pk])

            def pv(i, base=base):
                kt, m = items[i]
                p, pk = pT[(base + i) % 4]
                S.add("pe", mm(ps[2 * m][:, :], vb[:, kt, c0:c0 + 128], p, kt == 0, kt == 9), reads=[pk, vk], writes=[("ps", 2 * m)])
                S.add("pe", mm(ps[2 * m + 1][:, :], onesb[:], p, kt == 0, kt == 9), reads=[pk, "onesb"], writes=[("ps", 2 * m + 1)])

            for i in range(len(items) + LA):
                if i < len(items):
                    score(i)
                if i - LA >= 0:
                    pv(i - LA)
            S.add("dve", vrec(e1, ps[1][:, :]), reads=[("ps", 1)], writes=[e1k])
            S.add("dve", vtt(e1, ps[0][:, :], e1, ALU.mult), reads=[("ps", 0), e1k], writes=[e1k])
            S.add("dve", vrec(e2, ps[3][:, :]), reads=[("ps", 3)], writes=[e2k])
            S.add("dve", vtt(e2, ps[2][:, :], e2, ALU.mult), reads=[("ps", 2), e2k], writes=[e2k])
            S.add("dve", vstt(e1, e2, nlam, e1, ALU.mult, ALU.add), reads=[e1k, e2k, nlk], writes=[e1k])
            S.add("dve", vtt(e3, e1, e1, ALU.mult), reads=[e1k], writes=[e3k])
            S.add("pe", mm(ps[0][:, :], C("ones"), e3), reads=[e3k], writes=[("ps", 0)])
            S.add("dve", vts(e3, ps[0][:, :], 1.0 / 128, EPS, ALU.mult, ALU.add), reads=[("ps", 0)], writes=[e3k])
            S.add("act", act(e3, e3, AF.Sqrt), reads=[e3k], writes=[e3k])
            S.add("dve", vrec(e3, e3), reads=[e3k], writes=[e3k])
            S.add("dve", vtt(e1, e1, e3, ALU.mult), reads=[e1k, e3k], writes=[e1k])
            S.add("dve", vts(ob[:, qs], e1, C("gsubT%d" % l), 1.0 - li, ALU.mult, ALU.mult), reads=[e1k], writes=[obk])
        mix_store(h, ob, obk)

    def bcast_row(tensor_ap_1d):
        n = tensor_ap_1d.shape[0]
        return bass.AP(tensor_ap_1d.tensor, tensor_ap_1d.offset, [[0, 128], [1, n]])

    def qk_norm_fm(x32, xk, sq, sqk, gT):
        S.add("dve", vtt(sq, x32, x32, ALU.mult), reads=[xk], writes=[sqk])
        for th in range(2):
            sl = slice(th * 512, (th + 1) * 512)
            b = bank()
            S.add("pe", mm(ps[b][:, :], C("ones"), sq[:, sl]), reads=[sqk], writes=[("ps", b)])
            S.add("dve", vts(sq[:, sl], ps[b][:, :], 1.0 / 128, EPS, ALU.mult, ALU.add), reads=[("ps", b)], writes=[sqk])
        S.add("act", act(sq, sq, AF.Sqrt), reads=[sqk], writes=[sqk])
        S.add("dve", vrec(sq, sq), reads=[sqk], writes=[sqk])
        S.add("dve", vstt(x32, x32, gT, sq, ALU.mult, ALU.mult), reads=[xk, sqk], writes=[xk])

    def fm_to_32(dst, dk, b0=0, b1=1):
        S.add("act", act(dst[:, 0:512], ps[b0][:, :], AF.Copy), reads=[("ps", b0)], writes=[dk])
        S.add("act", act(dst[:, 512:1024], ps[b1][:, :], AF.Copy), reads=[("ps", b1)], writes=[dk])

    def attn_single(qTb, qTk, kTa, kTk, vb, vk, vcols, pT, scale, e1, e1k, ob, obk):
        LA = 2
        base0 = [0]
        for qh in range(2):
            qs = slice(qh * 512, (qh + 1) * 512)
            base = base0[0]
            base0[0] += 10

            def score(kt, qh=qh, qs=qs, base=base):
                sb = 4 + (base + kt) % 4
                p, pk = pT[(base + kt) % 4]
                S.add("pe", mm(ps[sb][:, :], kTa[:, kt * 128:(kt + 1) * 128], qTb[:, qs]), reads=[qTk, kTk], writes=[("ps", sb)])
                for qb in range(2):
                    o = kt * 4 + qh * 2 + qb
                    S.add("act", act(p[:, qb * 256:(qb + 1) * 256], ps[sb][:, qb * 256:(qb + 1) * 256], AF.Exp, bias=C("maskAB", o, o + 1), scale=scale),
                          reads=[("ps", sb)], writes=[pk])

            def pv(kt, base=base):
                p, pk = pT[(base + kt) % 4]
                S.add("pe", mm(ps[0][:, :], vb[:, kt, vcols], p, kt == 0, kt == 9), reads=[pk, vk], writes=[("ps", 0)])
                S.add("pe", mm(ps[1][:, :], onesb[:], p, kt == 0, kt == 9), reads=[pk, "onesb"], writes=[("ps", 1)])

            for i in range(10 + LA):
                if i < 10:
                    score(i)
                if i - LA >= 0:
                    pv(i - LA)
            S.add("dve", vrec(e1, ps[1][:, :]), reads=[("ps", 1)], writes=[e1k])
            S.add("dve", vtt(ob[:, qs], ps[0][:, :], e1, ALU.mult), reads=[("ps", 0), e1k], writes=[obk])

    def phase_mixB(l):
        pmark('phase_mixB')
        AR.reset()
        cosT, ck = AR.alloc([128, T], F32); sinT, _ = AR.alloc([128, T], F32)
        load(cosT, tabs_d[2], [ck]); load(sinT, tabs_d[3], [ck])
        q32, q32k = AR.alloc([128, T], F32); k32, k32k = AR.alloc([128, T], F32)
        t1, t1k = AR.alloc([128, T], F32); sq, sqk = AR.alloc([128, T], F32)
        qTb, qTk = AR.alloc([128, T], BF16)
        kTa = [AR.alloc([128, 1280], BF16) for _ in range(2)]
        vb, vk = AR.alloc([128, 10, 256], BF16)
        vst = [AR.alloc([128, 256], F32) for _ in range(2)]
        kst = [AR.alloc([128, 256], F32) for _ in range(2)]
        tm, tmk = AR.alloc([128, 256], F32)
        cks, cksk = AR.alloc([128, 2, 128], F32); cvs, cvsk = AR.alloc([128, 2, 256], F32)
        pT = [AR.alloc([128, 512], BF16) for _ in range(4)]
        e1, e1k = AR.alloc([128, 512], F32)
        ob, obk = AR.alloc([128, T], BF16)
        sk_ = wload(w_in[l][:, OKB:OKB + 256], 256)
        sv = wload(w_in[l][:, OVB:OVB + 256], 256)
        gk2 = bass.AP(cst, CL["gk_bc%d" % l][0], [[NCST, 128], [0, 2], [1, 128]])
        for tt in range(NT):
            b = bank()
            proj_tm(sk_, tt, 256, b)
            st, stk = kst[tt % 2]
            ss2 = small[:, 24 + 2 * (tt % 2):26 + 2 * (tt % 2)]
            ssk = ("ss2", tt % 2)
            S.add("dve", vcp(st, ps[b][:, 0:256]), reads=[("ps", b)], writes=[stk])
            S.add("dve", vtt(tm, st, st, ALU.mult), reads=[stk], writes=[tmk])
            S.add("dve", (lambda o=ss2, i=tm: nc.vector.reduce_sum(out=o, in_=i.rearrange("p (g d) -> p g d", d=128), axis=mybir.AxisListType.X)),
                  reads=[tmk], writes=[ssk])
            rstd_from_ss(ss2, 128, ssk, ssk)
            st3 = st.rearrange("p (g d) -> p g d", d=128)
            rb = bass.AP(small, 24 + 2 * (tt % 2), [[64, 128], [1, 2], [0, 128]])
            S.add("dve", vtt(st3, st3, rb, ALU.mult), reads=[stk, ssk], writes=[stk])
            S.add("dve", vtt(st3, st3, gk2, ALU.mult), reads=[stk], writes=[stk])
            out_keys.append(store(nbk[l, tt * 128:(tt + 1) * 128, :], st, [stk]))
            b = bank()
            proj_tm(sv, tt, 256, b)
            st, stk = vst[tt % 2]
            S.add("act", act(st, ps[b][:, 0:256], AF.Copy), reads=[("ps", b)], writes=[stk])
            S.add("dve", vcp(vb[:, 2 + tt, :], st), reads=[stk], writes=[vk])
            out_keys.append(store(nbv[l, tt * 128:(tt + 1) * 128, :], st, [stk]))
        load(cvs, cbv[l].rearrange("(a p) d -> p a d", p=128), [cvsk])
        S.add("dve", vcp(vb[:, 0:2, :], cvs), reads=[cvsk], writes=[vk])
        for g in range(2):
            kT_, kTk = kTa[g]
            proj_fm(sk_, g * 128, 0, 1)
            fm_to_32(k32, k32k)
            qk_norm_fm(k32, k32k, sq, sqk, C("gkT%d" % l))
            rope(k32, k32k, C("RBC"), cosT, sinT, ck, kT_[:, 256:1280], kTk, t1, t1k, 2, 3)
            cache_kT(cbk[l][:, g * 128:(g + 1) * 128], kT_[:, 0:256], kTk, cks, cksk)
        qTb2 = [(qTb, qTk), AR.alloc([128, T], BF16)]
        qslot = {}

        def prepB(h):
            if h % 2 == 0:
                qslot[h // 2] = wload(w_in[l][:, OQB + (h // 2) * 256:OQB + (h // 2 + 1) * 256], 256)
            q_, qk_ = qTb2[h % 2]
            proj_fm(qslot[h // 2], (h % 2) * 128, 0, 1)
            fm_to_32(q32, q32k)
            qk_norm_fm(q32, q32k, sq, sqk, C("gqT%d" % l))
            rope(q32, q32k, C("RBC"), cosT, sinT, ck, q_, qk_, t1, t1k, 2, 3)

        prepB(0)
        for h in range(8):
            if h + 1 < 8:
                prepB(h + 1)
            g = h // 4
            q_, qk_ = qTb2[h % 2]
            attn_single(q_, qk_, kTa[g][0], kTa[g][1], vb, vk, slice(g * 128, (g + 1) * 128), pT, 128 ** -0.5, e1, e1k, ob, obk)
            mix_store(8 + h, ob, obk)
        S.barrier()

    def phase_mixC(l):
        pmark('phase_mixC')
        AR.reset()
        cosT, ck = AR.alloc([128, T], F32); sinT, _ = AR.alloc([128, T], F32)
        load(cosT, tabs_d[2], [ck]); load(sinT, tabs_d[3], [ck])
        cm, cmk = AR.alloc([128, 2048], F32)
        load(cm, cmask_d, [cmk])
        q32, q32k = AR.alloc([128, T], F32); k32, k32k = AR.alloc([128, T], F32)
        t1, t1k = AR.alloc([128, T], F32)
        qTb, qTk = AR.alloc([128, T], BF16)
        kTa = [AR.alloc([128, 1280], BF16) for _ in range(2)]
        vb, vk = AR.alloc([128, 10, 256], BF16)
        vst = [AR.alloc([128, 256], F32) for _ in range(2)]
        kst = [AR.alloc([128, 256], F32) for _ in range(2)]
        cks, cksk = AR.alloc([128, 2, 128], F32); cvs, cvsk = AR.alloc([128, 2, 256], F32)
        pT = [AR.alloc([128, 128], BF16) for _ in range(4)]
        pf = [AR.alloc([128, 128], F32) for _ in range(2)]
        e1, e1k = AR.alloc([128, 128], F32)
        ob, obk = AR.alloc([128, T], BF16)
        esk = small[:, 32:40]
        S.add("act", act(esk, C("sink_bc%d" % l), AF.Exp), writes=["esk"])
        sk_ = wload(w_in[l][:, OKC:OKC + 256], 256)
        sv = wload(w_in[l][:, OVC:OVC + 256], 256)
        for tt in range(NT):
            b = bank()
            proj_tm(sk_, tt, 256, b)
            st, stk = kst[tt % 2]
            S.add("act", act(st, ps[b][:, 0:256], AF.Copy), reads=[("ps", b)], writes=[stk])
            out_keys.append(store(nck[l, tt * 128:(tt + 1) * 128, :], st, [stk]))
            b = bank()
            proj_tm(sv, tt, 256, b)
            st, stk = vst[tt % 2]
            S.add("act", act(st, ps[b][:, 0:256], AF.Copy), reads=[("ps", b)], writes=[stk])
            S.add("dve", vcp(vb[:, 2 + tt, :], st), reads=[stk], writes=[vk])
            out_keys.append(store(ncv[l, tt * 128:(tt + 1) * 128, :], st, [stk]))
        load(cvs, ccv[l].rearrange("(a p) d -> p a d", p=128), [cvsk])
        S.add("dve", vcp(vb[:, 0:2, :], cvs), reads=[cvsk], writes=[vk])
        for g in range(2):
            kT_, kTk = kTa[g]
            proj_fm(sk_, g * 128, 0, 1)
            fm_to_32(k32, k32k)
            rope(k32, k32k, C("RBC"), cosT, sinT, ck, kT_[:, 256:1280], kTk, t1, t1k, 2, 3)
            cache_kT(cck[l][:, g * 128:(g + 1) * 128], kT_[:, 0:256], kTk, cks, cksk)
        scale = 128 ** -0.5
        it = 0
        for qs_ in range(4):
            sq_ = wload(w_in[l][:, OQC + qs_ * 256:OQC + (qs_ + 1) * 256], 256)
            for hh in range(2):
                h = qs_ * 2 + hh
                g = h // 4
                kT_, kTk = kTa[g]
                proj_fm(sq_, hh * 128, 0, 1)
                fm_to_32(q32, q32k)
                rope(q32, q32k, C("RBC"), cosT, sinT, ck, qTb, qTk, t1, t1k, 2, 3)
                items = []
                for n in range(8):
                    tiles = [(0, "ctx"), (1, "ctx")]
                    if n > 0:
                        tiles.append((2 + n - 1, "prev"))
                    tiles.append((2 + n, "diag"))
                    if n < 7:
                        tiles.append((2 + n + 1, "next"))
                    for ti, (kt, kind) in enumerate(tiles):
                        items.append((n, kt, kind, ti == 0, ti == len(tiles) - 1))
                base = it
                it += len(items)

                def score(i, base=base, kT_=kT_, kTk=kTk):
                    n, kt, kind, first, last = items[i]
                    qs = slice(n * 128, (n + 1) * 128)
                    sb = 4 + (base + i) % 4
                    p, pk = pT[(base + i) % 4]
                    S.add("pe", mm(ps[sb][:, 0:128], kT_[:, kt * 128:(kt + 1) * 128], qTb[:, qs]), reads=[qTk, kTk], writes=[("ps", sb)])
                    if kind == "ctx":
                        S.add("act", act(p, ps[sb][:, 0:128], AF.Exp, bias=C("maskC"), scale=scale), reads=[("ps", sb)], writes=[pk])
                    elif kind == "diag":
                        S.add("act", act(p, ps[sb][:, 0:128], AF.Exp, scale=scale), reads=[("ps", sb)], writes=[pk])
                    else:
                        f_, fk = pf[(base + i) % 2]
                        S.add("act", act(f_, ps[sb][:, 0:128], AF.Exp, scale=scale), reads=[("ps", sb)], writes=[fk])
                        side = 0 if kind == "prev" else 1
                        mo = (n * 2 + side) * 128
                        S.add("dve", vtt(p, f_, cm[:, mo:mo + 128], ALU.mult), reads=[fk, cmk], writes=[pk])

                def pv(i, base=base, g=g, h=h):
                    n, kt, kind, first, last = items[i]
                    qs = slice(n * 128, (n + 1) * 128)
                    ob_, db_ = (0, 1) if n % 2 == 0 else (2, 3)
                    p, pk = pT[(base + i) % 4]
                    S.add("pe", mm(ps[ob_][:, 0:128], vb[:, kt, g * 128:(g + 1) * 128], p, first, last), reads=[pk, vk], writes=[("ps", ob_)])
                    S.add("pe", mm(ps[db_][:, 0:128], onesb[:], p, first, last), reads=[pk, "onesb"], writes=[("ps", db_)])
                    if last:
                        S.add("dve", vts(e1, ps[db_][:, 0:128], esk[:, h:h + 1], None, ALU.add), reads=[("ps", db_), "esk"], writes=[e1k])
                        S.add("dve", vrec(e1, e1), reads=[e1k], writes=[e1k])
                        S.add("dve", vtt(ob[:, qs], ps[ob_][:, 0:128], e1, ALU.mult), reads=[("ps", ob_), e1k], writes=[obk])

                LA = 2
                for i in range(len(items) + LA):
                    if i < len(items):
                        score(i)
                    if i - LA >= 0:
                        pv(i - LA)
                mix_store(16 + h, ob, obk)
        S.barrier()

    dtall = A("dtall", [128, NT, 32], F32)
    dtA = A("dtA", [128, NT, 32], F32)
    abc = A("abc", [128, 32], F32)
    dtot = A("dtot", [128, 16], F32)
    chq = A("chq", [128, 4, 32], F32)
    Hst = A("Hst", [128, 1024], F32)
    zs_h = dscr("zs_h", [T, 1024]); yp_h = dscr("yp_h", [T, 1024])

    def bc64(t, off, rowlen, nh=16):
        return bass.AP(t, off, [[rowlen, 128], [1, nh], [0, 64]])

    def phase_mixD(l):
        pmark('phase_mixD')
        AR.reset()
        xsT, xsk = AR.alloc([128, 8, T], BF16)
        BmT, bmk = AR.alloc([128, 2, T], BF16); CmT, cmk_ = AR.alloc([128, 2, T], BF16)
        mark = AR.off
        u32, uk = AR.alloc([128, 4, 260], F32); acc, ak = AR.alloc([128, 4, 256], F32)
        zst = [AR.alloc([128, 256], F32) for _ in range(2)]
        S.add("dve", vset(u32[:, 0, 0:2], 0.0), writes=[uk])
        S.add("dve", vset(u32[:, 3, 258:260], 0.0), writes=[uk])
        for sl_ in range(6):
            s = wload(w_in[l][:, OXBC + sl_ * 256:OXBC + (sl_ + 1) * 256], 256)
            for hh in range(2):
                cc = sl_ * 2 + hh
                proj_fm(s, hh * 128, 0, 1)
                for th, b in ((0, 0), (1, 1)):
                    S.add("act", act(u32[:, 2 * th:2 * th + 2, 2:258], ps[b][:, :].rearrange("p (a t) -> p a t", t=256), AF.Copy),
                          reads=[("ps", b)], writes=[uk])
                S.add("dve", vts(u32[:, 1:4, 0:2], u32[:, 0:3, 256:258], C("flag")), reads=[uk], writes=[uk])
                S.add("dve", vts(u32[:, 0:3, 258:260], u32[:, 1:4, 2:4], C("flag")), reads=[uk], writes=[uk])
                wo = CL["convw%d" % l][0] + cc * 5
                S.add("dve", vts(acc, u32[:, :, 0:256], cst[:, wo:wo + 1]), reads=[uk], writes=[ak])
                for k in range(1, 5):
                    S.add("dve", vstt(acc, u32[:, :, k:k + 256], cst[:, wo + k:wo + k + 1], acc, ALU.mult, ALU.add), reads=[uk, ak], writes=[ak])
                bo = CL["convb%d" % l][0] + cc
                if cc < 8:
                    dst = xsT[:, cc, :]; dk = xsk
                elif cc < 10:
                    dst = BmT[:, cc - 8, :]; dk = bmk
                else:
                    dst = CmT[:, cc - 10, :]; dk = cmk_
                S.add("act", act(dst.rearrange("p (a t) -> p a t", t=256), acc, AF.Silu, bias=cst[:, bo:bo + 1]), reads=[ak], writes=[dk])
        s = wload(w_in[l][:, ODT:ODT + 32], 32)
        for tt in range(NT):
            b = bank()
            proj_tm(s, tt, 32, b)
            S.add("dve", vtt(dtall[:, tt, :], ps[b][:, 0:32], C("dtb%d" % l), ALU.add), reads=[("ps", b)], writes=["dtall"])
        dflat = dtall[:].rearrange("p a b -> p (a b)")
        S.add("act", act(dflat, dflat, AF.Exp), reads=["dtall"], writes=["dtall"])
        S.add("act", act(dflat, dflat, AF.Ln, bias=1.0), reads=["dtall"], writes=["dtall"])
        S.add("act", act(abc[:], C("alog%d" % l), AF.Exp), writes=["abc"])
        S.add("dve", vts(abc[:], abc[:], -1.0), reads=["abc"], writes=["abc"])
        abc3 = bass.AP(abc, 0, [[32, 128], [0, NT], [1, 32]])
        S.add("dve", vtt(dtA[:], dtall[:], abc3, ALU.mult), reads=["dtall", "abc"], writes=["dtA"])
        S.add("dve", vtt(dtot[:], C("dsk%d" % l, 0, 16), C("dsk%d" % l, 16, 32), ALU.add), writes=["dtot"])
        zkeys = []
        for zi in range(4):
            s = wload(w_in[l][:, OZ + zi * 256:OZ + (zi + 1) * 256], 256)
            for tt in range(NT):
                b = bank()
                proj_tm(s, tt, 256, b)
                st, stk = zst[tt % 2]
                S.add("act", act(st, ps[b][:, 0:256], AF.Silu), reads=[("ps", b)], writes=[stk])
                zkeys.append(store(zs_h[tt * 128:(tt + 1) * 128, zi * 256:(zi + 1) * 256], st, [stk]))
        S.barrier()
        pmark('mixD_chunks')
        AR.off = mark
        xs_t, xstk = AR.alloc([128, 1024], BF16)
        xdt, xdtk = AR.alloc([128, 1024], BF16); xdw, xdwk = AR.alloc([128, 1024], BF16)
        bmt, bmtk = AR.alloc([128, 256], BF16)
        mcb = [AR.alloc([128, 128], F32) for _ in range(2)]
        lt = [AR.alloc([128, 128], F32) for _ in range(4)]
        ex = [AR.alloc([128, 128], F32) for _ in range(4)]
        MT = [AR.alloc([128, 128], BF16) for _ in range(4)]
        yf, yfk = AR.alloc([128, 1024], F32); y2, y2k = AR.alloc([128, 1024], F32)
        zl, zlk = AR.alloc([128, 1024], F32)
        Hb, Hbk = AR.alloc([128, 1024], BF16)
        hst, hstk = AR.alloc([128, 8, 128], F32)
        odb, odk = AR.alloc([128, 1024], BF16); odT, odTk = AR.alloc([128, 8, 128], BF16)
        gss, gsk = AR.alloc([128, 1024], F32)
        load(gss, gssm_d[l], [gsk])

        def load_H(d):
            load(hst, h0_d[l, d].rearrange("(a p) n -> p a n", p=128), [hstk])
            for half in range(2):
                b = 6 + half
                S.add("pe", seq([tr(ps[b][:, i * 128:(i + 1) * 128], hst[:, half * 4 + i, :], C("ident")) for i in range(4)]),
                      reads=[hstk], writes=[("ps", b)])
                S.add("act", act(Hst[:, half * 512:(half + 1) * 512], ps[b][:, :], AF.Copy), reads=[("ps", b)], writes=["Hst"])

        def store_H(sq_, d):
            for half in range(2):
                b = 6 + half
                S.add("pe", seq([tr(ps[b][:, i * 128:(i + 1) * 128], Hst[:, (half * 4 + i) * 128:(half * 4 + i + 1) * 128], C("ident")) for i in range(4)]),
                      reads=["Hst"], writes=[("ps", b)])
                S.add("act", act(hst[:, half * 4:half * 4 + 4, :], ps[b][:, :].rearrange("p (a n) -> p a n", n=128), AF.Copy),
                      reads=[("ps", b)], writes=[hstk])
            out_keys.append(store(nssm[l, sq_, d].rearrange("(a p) n -> p a n", p=128), hst, [hstk]))

        def chunk(c, d):
            cs = slice(c * 128, (c + 1) * 128)
            do = d * 16
            psb = ps[7][:, :].bitcast(BF16)
            S.add("pe", seq([tr(psb[:, i * 128:(i + 1) * 128], xsT[:, i, cs], identb[:]) for i in range(8)]), reads=[xsk], writes=[("ps", 7)])
            S.add("act", act(xs_t, psb[:, 0:1024], AF.Copy), reads=[("ps", 7)], writes=[xstk])
            S.add("pe", seq([tr(psb[:, i * 128:(i + 1) * 128], BmT[:, i, cs], identb[:]) for i in range(2)]), reads=[bmk, xstk], writes=[("ps", 7)])
            S.add("act", act(bmt, psb[:, 0:256], AF.Copy), reads=[("ps", 7)], writes=[bmtk])
            triX = C("triU") if d == 0 else C("triL")
            S.add("pe", mm(ps[6][:, 0:16], triX, dtA[:, c, do:do + 16]), reads=["dtA", bmtk], writes=[("ps", 6)])
            S.add("pe", mm(ps[6][:, 16:32], C("ones"), dtA[:, c, do:do + 16]), reads=["dtA"], writes=[("ps", 6)])
            S.add("dve", vcp(chq[:, 0, 0:32], ps[6][:, 0:32]), reads=[("ps", 6)], writes=["chq"])
            S.add("act", act(chq[:, 1, 0:16], chq[:, 0, 0:16], AF.Exp), reads=["chq"], writes=["chq"])
            S.add("dve", vtt(chq[:, 2, 0:16], chq[:, 0, 16:32], chq[:, 0, 0:16], ALU.subtract), reads=["chq"], writes=["chq"])
            S.add("act", act(chq[:, 2, 0:16], chq[:, 2, 0:16], AF.Exp), reads=["chq"], writes=["chq"])
            S.add("act", act(chq[:, 3, 0:16], chq[:, 0, 16:32], AF.Exp), reads=["chq"], writes=["chq"])
            x3 = xs_t.rearrange("p (h e) -> p h e", e=64)
            S.add("dve", vtt(xdt.rearrange("p (h e) -> p h e", e=64), x3, bc64(dtall, c * 32 + do, NT * 32), ALU.mult), reads=[xstk, "dtall"], writes=[xdtk])
            S.add("dve", vtt(xdw.rearrange("p (h e) -> p h e", e=64), xdt.rearrange("p (h e) -> p h e", e=64), bc64(chq, 2 * 32, 4 * 32), ALU.mult),
                  reads=[xdtk, "chq"], writes=[xdwk])
            trimask = C("triU") if d == 0 else C("triL")
            for g in range(2):
                S.add("pe", mm(ps[6][:, 128 + g * 128:256 + g * 128], BmT[:, g, cs], CmT[:, g, cs]), reads=[bmk, cmk_, "chq"], writes=[("ps", 6)])
                m_, mk_ = mcb[g]
                S.add("dve", vtt(m_, ps[6][:, 128 + g * 128:256 + g * 128], trimask, ALU.mult), reads=[("ps", 6)], writes=[mk_])
            sX = C("sL") if d == 0 else C("sU")
            def seg_part(h):
                lt_, ltk = lt[h % 4]; ex_, exk = ex[h % 4]
                S.add("dve", vts(lt_, sX, dtA[:, c, do + h:do + h + 1]), reads=["dtA"], writes=[ltk])
                sb = 4 + (h % 2)
                sc0 = ((h // 2) % 4) * 128
                S.add("pe", mm(ps[sb][:, sc0:sc0 + 128], lt_, triX), reads=[ltk], writes=[("ps", sb)])
                S.add("act", act(ex_, ps[sb][:, sc0:sc0 + 128], AF.Exp), reads=[("ps", sb)], writes=[exk])

            def yd_part(h):
                g = h // 8
                ex_, exk = ex[h % 4]; M_, Mk = MT[h % 4]
                S.add("dve", vtt(M_, ex_, mcb[g][0], ALU.mult), reads=[exk, mcb[g][1]], writes=[Mk])
                S.add("pe", mm(ps[h // 8][:, (h % 8) * 64:(h % 8 + 1) * 64], M_, xdt[:, h * 64:(h + 1) * 64]), reads=[Mk, xdtk], writes=[("ps", h // 8)])

            LA = 2
            for i in range(16 + LA):
                if i < 16:
                    seg_part(i)
                if i - LA >= 0:
                    yd_part(i - LA)
            for g in range(2):
                S.add("pe", mm(ps[2 + g][:, :], CmT[:, g, cs], Hb[:, g * 512:(g + 1) * 512]), reads=[cmk_, Hbk], writes=[("ps", 2 + g)])
            for g in range(2):
                sl = slice(g * 512, (g + 1) * 512)
                S.add("dve", vtt(yf[:, sl].rearrange("p (h e) -> p h e", e=64), ps[2 + g][:, :].rearrange("p (h e) -> p h e", e=64),
                                 bc64(chq, 32 + g * 8, 4 * 32, nh=8), ALU.mult), reads=[("ps", 2 + g), "chq"], writes=[yfk])
                S.add("dve", vtt(yf[:, sl], yf[:, sl], ps[g][:, :], ALU.add), reads=[("ps", g), yfk], writes=[yfk])
            for g in range(2):
                sl = slice(g * 512, (g + 1) * 512)
                S.add("pe", mm(ps[2 + g][:, :], bmt[:, g * 128:(g + 1) * 128], xdw[:, sl]), reads=[bmtk, xdwk, yfk], writes=[("ps", 2 + g)])
                S.add("dve", vtt(Hst[:, sl].rearrange("p (h e) -> p h e", e=64), Hst[:, sl].rearrange("p (h e) -> p h e", e=64),
                                 bc64(chq, 3 * 32 + g * 8, 4 * 32, nh=8), ALU.mult), reads=["Hst", "chq", Hbk], writes=["Hst"])
                S.add("dve", vtt(Hst[:, sl], Hst[:, sl], ps[2 + g][:, :], ALU.add), reads=["Hst", ("ps", 2 + g)], writes=["Hst"])

        def set_Hb():
            S.add("act", act(Hb, Hst[:], AF.Copy), reads=["Hst"], writes=[Hbk])

        load_H(0)
        ypk = {}
        for c in range(NT):
            if c >= 2 and c % 2 == 0:
                S.add("dve", vts(Hst[:], Hst[:], C("flag")), reads=["Hst"], writes=["Hst"])
            set_Hb()
            chunk(c, 0)
            ypk[c] = store(yp_h[c * 128:(c + 1) * 128, :], yf, [yfk])
            if c % 2 == 1:
                store_H(c // 2, 0)
        load_H(1)
        for c in range(NT - 1, -1, -1):
            if c <= 5 and c % 2 == 1:
                S.add("dve", vts(Hst[:], Hst[:], C("flag")), reads=["Hst"], writes=["Hst"])
            set_Hb()
            chunk(c, 1)
            if c % 2 == 0:
                store_H(c // 2, 1)
            load(y2, yp_h[c * 128:(c + 1) * 128, :], [y2k], rkeys=[ypk[c]])
            load(zl, zs_h[c * 128:(c + 1) * 128, :], [zlk], rkeys=zkeys)
            S.add("dve", vtt(y2, y2, yf, ALU.add), reads=[y2k, yfk], writes=[y2k])
            S.add("dve", vtt(yf.rearrange("p (h e) -> p h e", e=64), xs_t.rearrange("p (h e) -> p h e", e=64), bc64(dtot, 0, 16), ALU.mult),
                  reads=[xstk, "dtot", y2k], writes=[yfk])
            S.add("dve", vtt(y2, y2, yf, ALU.add), reads=[y2k, yfk], writes=[y2k])
            S.add("dve", vtt(y2, y2, zl, ALU.mult), reads=[y2k, zlk], writes=[y2k])
            S.add("dve", vtt(yf, y2, y2, ALU.mult), reads=[y2k], writes=[yfk])
            ssg = small[:, 40:42]
            S.add("dve", (lambda o=ssg, i=yf: nc.vector.reduce_sum(out=o, in_=i.rearrange("p (g d) -> p g d", d=512), axis=mybir.AxisListType.X)),
                  reads=[yfk], writes=["ssg"])
            rstd_from_ss(ssg, 512, "ssg", "ssg")
            S.add("dve", vtt(y2.rearrange("p (g d) -> p g d", d=512), y2.rearrange("p (g d) -> p g d", d=512),
                             bass.AP(small, 40, [[64, 128], [1, 2], [0, 512]]), ALU.mult), reads=[y2k, "ssg"], writes=[y2k])
            S.add("dve", vtt(odb, y2, gss, ALU.mult), reads=[y2k, gsk], writes=[odk])
            psb = ps[7][:, :].bitcast(BF16)
            S.add("pe", seq([tr(psb[:, i * 128:(i + 1) * 128], odb[:, i * 128:(i + 1) * 128], identb[:]) for i in range(8)]), reads=[odk], writes=[("ps", 7)])
            S.add("act", act(odT, psb[:, 0:1024].rearrange("p (a t) -> p a t", t=128), AF.Copy), reads=[("ps", 7)], writes=[odTk])
            out_keys.append(store(mixT_h[24 * 128:32 * 128, c * 128:(c + 1) * 128].rearrange("(a p) t -> p a t", p=128), odT, [odTk]))
        S.barrier()

    def phase_op(l):
        pmark('phase_op')
        AR.reset()
        for q in range(4):
            load(bufA[:, q * 8:(q + 1) * 8, :], mixT_h[q * 1024:(q + 1) * 1024, :].rearrange("(j p) t -> p j t", p=128), ["hTall"])
        sts = [AR.alloc([128, 256], F32) for _ in range(4)]
        i = 0
        for cb in range(16):
            s = wload(w_out[l][:, cb * 256:(cb + 1) * 256], 256)
            for tt in range(NT):
                b = bank()
                proj_tm(s, tt, 256, b)
                st, stk = sts[i % 4]
                i += 1
                if i % 2 == 0:
                    S.add("act", act(st, ps[b][:, 0:256], AF.Copy), reads=[("ps", b)], writes=[stk])
                else:
                    S.add("dve", vcp(st, ps[b][:, 0:256]), reads=[("ps", b)], writes=[stk])
                store(ymix[tt * 128:(tt + 1) * 128, cb * 256:(cb + 1) * 256], st, [stk])
        S.barrier()

    def gtg_build(l, v, gp_d, gt, gtk, tmp, tmpk):
        load(gt, bcast_row(modraw[l, v * D:(v + 1) * D]), [gtk])
        load(tmp, bcast_row(bmod_d[l, v * D:(v + 1) * D]), [tmpk])
        S.add("dve", vtt(gt, gt, tmp, ALU.add), reads=[gtk, tmpk], writes=[gtk])
        load(tmp, bcast_row(gp_d[l]), [tmpk])
        S.add("dve", vtt(gt, gt, tmp, ALU.mult), reads=[gtk, tmpk], writes=[gtk])

    def phase_resid(l, v, gp_d, x_src, x_dst, to_h2):
        pmark('phase_resid')
        AR.reset()
        gt, gtk = AR.alloc([128, D], F32)
        xt, xk = AR.alloc([128, D], F32)
        yt, yk = AR.alloc([128, D], F32)
        junk, jk = AR.alloc([128, D], BF16)
        gtg_build(l, v, gp_d, gt, gtk, xt, xk)
        for tt in range(NT):
            rows = slice(tt * 128, (tt + 1) * 128)
            load(yt, ymix[rows, :], [yk])
            load(xt, x_src[rows, :], [xk])
            ss = small[:, tt:tt + 1]; sk = ("ss", tt)
            sumsq(yt, yk, junk, jk, ss, sk)
            rstd_from_ss(ss, D, sk, sk)
            S.add("dve", vstt(yt, yt, ss, gt, ALU.mult, ALU.mult), reads=[yk, sk, gtk], writes=[yk])
            S.add("dve", vtt(xt, xt, yt, ALU.add), reads=[xk, yk], writes=[xk])
            k = store(x_dst[rows, :], xt, [xk])
            if x_dst is y_d:
                out_keys.append(k)
            if to_h2:
                ss2 = small[:, 8 + tt:9 + tt]; sk2 = ("ssb", tt)
                sumsq(xt, xk, junk, jk, ss2, sk2)
                rstd_from_ss(ss2, D, sk2, sk2)
                S.add("dve", vts(yt, xt, ss2), reads=[xk, sk2, yk], writes=[yk])
                norm_to_hT(yt, yk, tt, gmod2T, modT[:, 96:128])
        S.barrier()

    def phase_ffn_up(l):
        pmark('phase_ffn_up')
        AR.reset()
        ust = [AR.alloc([128, 2, T], BF16) for _ in range(2)]
        rl = [AR.alloc([128, 512], F32) for _ in range(2)]
        i = 0
        for fb in range(64):
            s = wload(w_up[l][:, fb * 256:(fb + 1) * 256], 256)
            u_, ukk = ust[fb % 2]
            for hh in range(2):
                b0 = bank(); b1 = bank()
                proj_fm(s, hh * 128, b0, b1)
                for th, b in ((0, b0), (1, b1)):
                    r_, rk_ = rl[i % 2]
                    i += 1
                    S.add("act", act(r_, ps[b][:, :], AF.Relu), reads=[("ps", b)], writes=[rk_])
                    S.add("dve", vtt(u_[:, hh, th * 512:(th + 1) * 512], ps[b][:, :], r_, ALU.mult), reads=[("ps", b), rk_], writes=[ukk])
            store(uT_h[fb * 256:(fb + 1) * 256, :].rearrange("(c p) t -> p c t", p=128), u_, [ukk])
        S.barrier()

    def phase_ffn_down(l):
        pmark('phase_ffn_down')
        AR.reset()
        usl = [AR.alloc([128, 16, T], BF16) for _ in range(2)]
        sts = [A("dst%d" % i, [128, 512], F32) for i in range(2)] if "dst" not in _once else _once["dst"]
        _once["dst"] = sts
        ui = 0
        for cg in range(8):
            for fg in range(8):
                s = cnt["ring"] % NRING
                cnt["ring"] += 1
                wv = ring[s][:].rearrange("p j n -> p (j n)").rearrange("p (a b) -> p a b", b=512)
                S.add("pool", dma(wv, w_down[l][fg * 2048:(fg + 1) * 2048, cg * 512:(cg + 1) * 512].rearrange("(j p) n -> p j n", p=128), q="pool"),
                      writes=[("ring", s)], chan=ch_ring[s])
                u_, ukk = usl[ui % 2]
                ui += 1
                load(u_, uT_h[fg * 2048:(fg + 1) * 2048, :].rearrange("(j p) t -> p j t", p=128), [ukk])
                for tt in range(NT):
                    S.add("pe", seq([mm(ps[tt][:, :], u_[:, jj, tt * 128:(tt + 1) * 128], wv[:, jj, :],
                                        fg == 0 and jj == 0, fg == 7 and jj == 15) for jj in range(16)]),
                          reads=[ukk, ("ring", s)], writes=[("ps", tt)])
            for tt in range(NT):
                st = sts[tt % 2]
                stk = ("dst", tt % 2)
                if tt % 2 == 0:
                    S.add("act", act(st[:], ps[tt][:, :], AF.Copy), reads=[("ps", tt)], writes=[stk])
                else:
                    S.add("dve", vcp(st[:], ps[tt][:, :]), reads=[("ps", tt)], writes=[stk])
                store(ymix[tt * 128:(tt + 1) * 128, cg * 512:(cg + 1) * 512], st[:], [stk])
        S.barrier()

    _once = {}

    def done(name):
        return stop_after == name

    for l in range(2):
        if "skipmod" not in dbg:
            phase_mod(l)
        if l == 0 and done("mod"):
            dump("modT", modT[:], [], [128, 192])
            return finish()
        phase_n1(l, x_in if l == 0 else xl1)
        if l == 0 and done("n1"):
            for j in (0, 31):
                dump("hT%d" % j, bufA[:, j, :], [], [128, T])
            return finish()
        S.add("dve", vset(small[:, 20:21], 0.0), writes=["hTall"])
        S.barrier()
        for nm, ph in (("mixA", phase_mixA), ("mixB", phase_mixB), ("mixC", phase_mixC), ("mixD", phase_mixD)):
            if ("only_" in " ".join(dbg)) and ("only_" + nm) not in dbg:
                continue
            ph(l)
            if l == 0 and done(nm):
                dump("small", small[:], [], [128, 64])
                if "mixT" in dbg:
                    d_ = dout("dbg_mixT", [D, T], BF16)
                    dbg_out["mixT"] = d_
                    out_keys.append(store(d_, mixT_h, []))
                return finish()
        phase_op(l)
        if l == 0 and done("op"):
            return finish()
        phase_resid(l, 2, gpost_d, x_in if l == 0 else xl1, xmid, True)
        if l == 0 and done("n2"):
            return finish()
        phase_ffn_up(l)
        phase_ffn_down(l)
        phase_resid(l, 5, gpostf_d, xmid, xl1 if l == 0 else y_d, False)
        if l == 0 and done("l0"):
            for nm_, t_ in (("xl1", xl1), ("xmid", xmid)):
                if nm_ in dbg:
                    d_ = dout("dbg_" + nm_, [T, D])
                    for q in range(4):
                        out_keys.append(store(d_[q * 256:(q + 1) * 256, :], t_[q * 256:(q + 1) * 256, :], []))
            return finish()
    S.add("sp", None, reads=out_keys)
    return finish()


def _rope_tables(identity):
    tabs = np.zeros((4, 128, T), np.float32)
    t = np.arange(T)
    row = (t // 64).astype(np.float64)
    col = (t % 64).astype(np.float64)

    def fill(cosT, sinT, R, base, d):
        da = d // 2
        inv = 1.0 / (10000.0 ** (np.arange(0, da, 2, dtype=np.float64) / da))
        hf = da // 2
        for part, pos in ((0, row), (1, col)):
            o = base + part * da
            ang = pos[None, :] * inv[:, None]
            cs, sn = np.cos(ang), np.sin(ang)
            cosT[o:o + hf] = cs; cosT[o + hf:o + da] = cs
            sinT[o:o + hf] = sn; sinT[o + hf:o + da] = sn
            for i in range(hf):
                R[o + hf + i, o + i] = -1.0
                R[o + i, o + hf + i] = 1.0
    RA = np.zeros((128, 128), np.float32); RBC = np.zeros((128, 128), np.float32)
    fill(tabs[0], tabs[1], RA, 0, 64); fill(tabs[0], tabs[1], RA, 64, 64)
    fill(tabs[2], tabs[3], RBC, 0, 128)
    if identity:
        tabs[0] = 1.0; tabs[1] = 0.0; tabs[2] = 1.0; tabs[3] = 0.0
    return tabs, RA, RBC


def _prep_inputs(inp):
    f = np.float32
    lay = CL
    shared = {k: np.ascontiguousarray(inp[k], dtype=f) for k in ("w_mod", "w_in", "w_out", "w_up", "w_down", "b_mod", "g_post_mix", "g_post_ffn")}
    gssm = np.ascontiguousarray(np.broadcast_to(np.asarray(inp["g_ssm_norm"], f)[:, None, :], (2, 128, 1024)))
    maps = []
    ii = np.arange(128)
    for core in range(8):
        sample = core < 4
        cst = np.zeros((128, NCST), f)

        def put(name, arr):
            o, n = lay[name]
            cst[:, o:o + n] = arr
        put("ident", np.eye(128, dtype=f))
        put("triU", (ii[:, None] <= ii[None, :]).astype(f)); put("triL", (ii[:, None] >= ii[None, :]).astype(f))
        put("sL", (ii[:, None] > ii[None, :]).astype(f)); put("sU", (ii[:, None] < ii[None, :]).astype(f))
        put("ones", np.ones((128, 128), f))
        tabs, RA, RBC = _rope_tables(identity=not sample)
        put("RA", RA); put("RBC", RBC)
        put("flag", 1.0 if sample else 0.0)
        mab = np.zeros((10, 4), f)
        if not sample:
            mab[:] = NEG
            for kt in range(2, 10):
                seq_k = (kt - 2) // 2
                mab[kt, seq_k] = 0.0
        put("maskAB", np.broadcast_to(mab.reshape(1, 40), (128, 40)))
        put("maskC", 0.0 if sample else NEG)
        for l in range(2):
            put("bmodT%d" % l, np.asarray(inp["b_mod"], f)[l].reshape(192, 128).T)
            for g, nm in (("gpre", "g_pre_mix"), ("gpost", "g_post_mix"), ("gpref", "g_pre_ffn"), ("gpostf", "g_post_ffn")):
                put("%sT%d" % (g, l), np.asarray(inp[nm], f)[l].reshape(32, 128).T)
            put("gsubT%d" % l, np.asarray(inp["g_subln"], f)[l].reshape(128, 1))
            put("gqT%d" % l, np.asarray(inp["g_qnorm"], f)[l].reshape(128, 1))
            put("gkT%d" % l, np.asarray(inp["g_knorm"], f)[l].reshape(128, 1))
            put("gk_bc%d" % l, np.broadcast_to(np.asarray(inp["g_knorm"], f)[l][None, :], (128, 128)))
            put("sink_bc%d" % l, np.broadcast_to(np.asarray(inp["sink"], f)[l][None, :], (128, 8)))
            for a, nm in (("lq1", "lam_q1"), ("lk1", "lam_k1"), ("lq2", "lam_q2"), ("lk2", "lam_k2")):
                put("%s_%d" % (a, l), np.broadcast_to(np.asarray(inp[nm], f)[l][None, :], (128, 64)))
            cw = np.asarray(inp["conv_w"], f)[l]
            put("convw%d" % l, cw.reshape(5, 12, 128).transpose(2, 1, 0).reshape(128, 60))
            put("convb%d" % l, np.asarray(inp["conv_b"], f)[l].reshape(12, 128).T)
            put("dtb%d" % l, np.broadcast_to(np.asarray(inp["dt_bias"], f)[l].reshape(1, 32), (128, 32)))
            put("alog%d" % l, np.broadcast_to(np.asarray(inp["a_log"], f)[l].reshape(1, 32), (128, 32)))
            put("dsk%d" % l, np.broadcast_to(np.asarray(inp["d_skip"], f)[l].reshape(1, 32), (128, 32)))
        cm = np.zeros((128, 8, 2, 128), f)
        for n in range(8):
            if sample:
                cm[:, n, 0, :] = (ii[None, :] <= ii[:, None])
                cm[:, n, 1, :] = (ii[:, None] <= ii[None, :])
            else:
                cm[:, n, 0, :] = 1.0 if n % 2 == 1 else 0.0
                cm[:, n, 1, :] = 1.0 if n % 2 == 0 else 0.0
        m = dict(shared)
        m["cst"] = cst; m["tabs"] = tabs; m["cmask"] = cm.reshape(128, 2048); m["gssm"] = gssm
        if sample:
            b = core
            m["x_in"] = np.ascontiguousarray(inp["x_sample"][b], f)
            m["cT"] = np.ascontiguousarray(np.asarray(inp["c"], f)[b].reshape(32, 128).T)
            m["cak"] = np.ascontiguousarray(np.asarray(inp["cache_a_k"], f)[b].reshape(2, 256, 1024))
            m["cav"] = np.ascontiguousarray(np.asarray(inp["cache_a_v"], f)[b].reshape(2, 256, 1024))
            m["cbk"] = np.ascontiguousarray(np.asarray(inp["cache_b_k"], f)[b].reshape(2, 256, 256))
            m["cbv"] = np.ascontiguousarray(np.asarray(inp["cache_b_v"], f)[b].reshape(2, 256, 256))
            m["cck"] = np.ascontiguousarray(np.asarray(inp["cache_c_k"], f)[b].reshape(2, 256, 256))
            m["ccv"] = np.ascontiguousarray(np.asarray(inp["cache_c_v"], f)[b].reshape(2, 256, 256))
            m["h0"] = np.ascontiguousarray(np.asarray(inp["state_ssm"], f)[b].reshape(2, 2, 1024, 128))
        else:
            s0 = (core - 4) * 4
            m["x_in"] = np.ascontiguousarray(np.asarray(inp["x_prompt"], f)[s0:s0 + 4].reshape(T, D))
            m["cT"] = np.ascontiguousarray(np.asarray(inp["c_ctx"], f).reshape(32, 128).T)
            for k, w in (("cak", 1024), ("cav", 1024), ("cbk", 256), ("cbv", 256), ("cck", 256), ("ccv", 256)):
                m[k] = np.zeros((2, 256, w), f)
            m["h0"] = np.zeros((2, 2, 1024, 128), f)
        maps.append(m)
    return maps


_CACHE = {}


def kernel(**inputs):
    maps = _prep_inputs(inputs)
    if "nc" not in _CACHE:
        _CACHE["nc"] = build()[0]
    res = run_bass_kernel_spmd(_CACHE["nc"], maps, core_ids=list(range(8)))
    r = res.results
    f = np.float32
    y_sample = np.stack([r[c]["y"] for c in range(4)], 0).astype(f)
    y_prompt = np.concatenate([r[c]["y"].reshape(4, 256, D) for c in range(4, 8)], 0).astype(f)

    def gather(name, shp):
        parts = []
        for c in range(4, 8):
            a = r[c][name]
            a = a.reshape(2, 4, 256, -1).transpose(1, 0, 2, 3)
            parts.append(a)
        return np.ascontiguousarray(np.concatenate(parts, 0).reshape(shp)).astype(f)
    nak_ = gather("nak", (16, 2, 256, 8, 128)); nav_ = gather("nav", (16, 2, 256, 8, 128))
    nbk_ = gather("nbk", (16, 2, 256, 2, 128)); nbv_ = gather("nbv", (16, 2, 256, 2, 128))
    nck_ = gather("nck", (16, 2, 256, 2, 128)); ncv_ = gather("ncv", (16, 2, 256, 2, 128))
    ss = []
    for c in range(4, 8):
        a = r[c]["nssm"]
        ss.append(a.transpose(1, 0, 2, 3, 4))
    nssm_ = np.ascontiguousarray(np.concatenate(ss, 0).reshape(16, 2, 2, 16, 64, 128)).astype(f)
    return (y_prompt, y_sample, nak_, nav_, nbk_, nbv_, nck_, ncv_, nssm_)
```

```python
import math
import numpy as np
import concourse.bass as bass
import concourse.mybir as mybir
from concourse.bass_utils import run_bass_kernel_spmd

F32 = mybir.dt.float32
BF16 = mybir.dt.bfloat16
AF = mybir.ActivationFunctionType
ALU = mybir.AluOpType

T = 1024
D = 4096
NJ = 32
NT = 8
DFF = 16384
INW = 8736
EPS = 1e-6
NEG = -30000.0
OQA, OKA, OVA = 0, 1024, 2048
OQB, OKB, OVB = 3072, 4096, 4352
OQC, OKC, OVC = 4608, 5632, 5888
OZ, OXBC, ODT = 6144, 7168, 8704


class Chan:
    def __init__(self, sem):
        self.sem = sem
        self.count = 0
        self.last = None


class Op:
    __slots__ = ("eng", "fn", "chan", "deps", "signal", "sig_idx", "dma_count")

    def __init__(self, eng, fn, chan):
        self.eng = eng
        self.fn = fn
        self.chan = chan
        self.deps = ()
        self.signal = False
        self.sig_idx = 0
        self.dma_count = 0


class Sched:
    ENG = {"pe": "tensor", "act": "scalar", "dve": "vector", "pool": "gpsimd", "sp": "sync"}

    def __init__(self, nc):
        self.nc = nc
        self.ops = []
        self.last_w = {}
        self.readers = {}
        self.esem = {e: nc.alloc_semaphore(name="sem_" + e) for e in ("pe", "act", "dve", "pool", "sp")}
        self.chans = []
        self.last_on = {}

    def chan(self):
        c = Chan(self.nc.alloc_semaphore(name="dch%d" % len(self.chans)))
        self.chans.append(c)
        return c

    def add(self, eng, fn, reads=(), writes=(), chan=None, extra=()):
        op = Op(eng, fn, chan)
        pr = [k for k in reads if isinstance(k, tuple) and k[0] == "ps"]
        if pr:
            reads = [k for k in reads if k not in pr]
            writes = list(writes) + pr
        deps = {}
        for k in reads:
            w = self.last_w.get(k)
            if w is not None:
                deps[id(w)] = w
        for k in writes:
            w = self.last_w.get(k)
            if w is not None:
                deps[id(w)] = w
            rd = self.readers.get(k)
            if rd:
                for r in rd.values():
                    deps[id(r)] = r
        for d in extra:
            deps[id(d)] = d
        dl = []
        for d in deps.values():
            if d.chan is not None or d.eng != eng or eng in ("act", "dve"):
                d.signal = True
                dl.append(d)
        op.deps = dl
        rk = id(chan) if chan is not None else eng
        for k in reads:
            self.readers.setdefault(k, {})[rk] = op
        for k in writes:
            self.last_w[k] = op
            self.readers[k] = {}
        self.ops.append(op)
        if fn is not None:
            if chan is not None:
                chan.last = op
            else:
                self.last_on[eng] = op
        return op

    def barrier(self):
        keep = set(id(c) for c in getattr(self, "keep_chans", ()))
        lasts = [o for e, o in self.last_on.items() if e != "pool"]
        lasts += [c.last for c in self.chans if c.last is not None and id(c) not in keep]
        for e in ("pe", "act", "dve", "sp"):
            self.add(e, None, extra=lasts)
        self.last_w = {k: v for k, v in self.last_w.items() if isinstance(k, tuple) and k[0] == "ring"}
        self.readers = {k: v for k, v in self.readers.items() if isinstance(k, tuple) and k[0] == "ring"}

    def emit(self):
        nc = self.nc
        sig = {e: 0 for e in self.esem}
        waited = {}
        nwait = 0
        for op in self.ops:
            e = getattr(nc, self.ENG[op.eng])
            for d in op.deps:
                if d.chan is not None:
                    sem, val = d.chan.sem, d.dma_count
                else:
                    sem, val = self.esem[d.eng], d.sig_idx
                assert val > 0
                key = (op.eng, sem.num)
                if waited.get(key, 0) < val:
                    e.wait_ge(sem, val)
                    waited[key] = val
                    nwait += 1
            if op.fn is None:
                continue
            inst = op.fn()
            if op.chan is not None:
                op.chan.count += 16
                op.dma_count = op.chan.count
                inst.then_inc(op.chan.sem, 16)
            elif op.signal:
                sig[op.eng] += 1
                op.sig_idx = sig[op.eng]
                inst.then_inc(self.esem[op.eng], 1)
        return dict(n_ops=len(self.ops), n_wait=nwait, sig=sig)


def seq(fns):
    def f():
        r = None
        for g in fns:
            r = g()
        return r
    return f


def cst_layout():
    lay = {}
    off = [0]

    def put(name, n):
        lay[name] = (off[0], n)
        off[0] += n
    put("ident", 128); put("triU", 128); put("triL", 128); put("sL", 128); put("sU", 128); put("ones", 128)
    put("RA", 128); put("RBC", 128)
    put("flag", 1)
    put("maskAB", 40); put("maskC", 1)
    for l in range(2):
        put("bmodT%d" % l, 192)
        for g in ("gpre", "gpost", "gpref", "gpostf"):
            put("%sT%d" % (g, l), 32)
        put("gsubT%d" % l, 1); put("gqT%d" % l, 1); put("gkT%d" % l, 1)
        put("gk_bc%d" % l, 128)
        put("sink_bc%d" % l, 8)
        put("lq1_%d" % l, 64); put("lk1_%d" % l, 64); put("lq2_%d" % l, 64); put("lk2_%d" % l, 64)
        put("convw%d" % l, 60); put("convb%d" % l, 12)
        put("dtb%d" % l, 32); put("alog%d" % l, 32); put("dsk%d" % l, 32)
    return lay, off[0]


CL, NCST = cst_layout()


def build(stop_after=None, dbg=(), lite=(), nl=2):
    nc = bass.Bass("TRN2", target_bir_lowering=False)
    S = Sched(nc)

    used_in = []

    def din(name, shape, dt=F32):
        used_in.append(name)
        if name in lite:
            shape = [1] * len(shape)
        return nc.dram_tensor(name, list(shape), dt, kind="ExternalInput").ap()

    def dout(name, shape, dt=F32):
        return nc.dram_tensor(name, list(shape), dt, kind="ExternalOutput").ap()

    def dscr(name, shape, dt=F32):
        return nc.dram_tensor(name, list(shape), dt, kind="Internal").ap()

    x_in = din("x_in", [T, D]); cT_d = din("cT", [128, 32]); cst_d = din("cst", [128, NCST])
    tabs_d = din("tabs", [4, 128, T]); cmask_d = din("cmask", [128, 2048]); gssm_d = din("gssm", [2, 128, 1024])
    w_mod = din("w_mod", [nl, D, 6 * D]); w_in = din("w_in", [nl, D, INW]); w_out = din("w_out", [nl, D, D])
    w_up = din("w_up", [nl, D, DFF]); w_down = din("w_down", [nl, DFF, D])
    bmod_d = din("b_mod", [2, 6 * D]); gpost_d = din("g_post_mix", [2, D]); gpostf_d = din("g_post_ffn", [2, D])
    cak = din("cak", [2, 256, 1024]); cav = din("cav", [2, 256, 1024])
    cbk = din("cbk", [2, 256, 256]); cbv = din("cbv", [2, 256, 256])
    cck = din("cck", [2, 256, 256]); ccv = din("ccv", [2, 256, 256])
    h0_d = din("h0", [2, 2, 1024, 128])
    y_d = dout("y", [T, D])
    nak = dout("nak", [2, T, 1024]); nav = dout("nav", [2, T, 1024])
    nbk = dout("nbk", [2, T, 256]); nbv = dout("nbv", [2, T, 256])
    nck = dout("nck", [2, T, 256]); ncv = dout("ncv", [2, T, 256])
    nssm = dout("nssm", [2, 4, 2, 1024, 128])
    ymix = dscr("ymix", [T, D]); xmid = dscr("xmid", [T, D]); xl1 = dscr("xl1", [T, D])
    uT_h = dscr("uT_h", [DFF, T], BF16); mixT_h = dscr("mixT_h", [D, T], BF16); modraw = dscr("modraw", [2, 6 * D])
    dbg_out = {}

    def A(name, shape, dt):
        return nc.alloc_sbuf_tensor("sb_" + name, shape, dt)
    bufA = A("bufA", [128, NJ, T], BF16)
    bufB = A("bufB", [128, 32768], BF16)
    NRING = 3
    ring = [A("ring%d" % i, [128, NJ, 256], BF16) for i in range(NRING)]
    cst = A("cst", [128, NCST], F32)
    modT = A("modT", [128, 192], F32)
    gmod1T = A("gmod1T", [128, 32], F32); gmod2T = A("gmod2T", [128, 32], F32)
    sT = A("sT", [128, 32], BF16)
    small = A("small", [128, 64], F32)
    identb = A("identb", [128, 128], BF16); onesb = A("onesb", [128, 128], BF16)
    ps = [nc.alloc_psum_tensor("ps%d" % i, [128, 512], F32) for i in range(8)]

    def C(name, a=None, b=None):
        o, n = CL[name]
        if a is None:
            return cst[:, o:o + n]
        return cst[:, o + a:o + b]

    class Arena:
        def __init__(self):
            self.off = 0
            self.n = 0

        def reset(self):
            self.off = 0

        def alloc(self, shape, dt):
            n = 1
            for s in shape[1:]:
                n *= s
            nb = n * (4 if dt == F32 else 2)
            nb = (nb + 31) // 32 * 32
            assert self.off + nb <= 65536, ("arena overflow", self.off, nb)
            v = bufB[:, self.off // 2:(self.off + nb) // 2]
            self.off += nb
            if dt == F32:
                v = v.bitcast(F32)
            v = v[:, 0:n]
            if len(shape) == 3:
                v = v.rearrange("p (a b) -> p a b", b=shape[2])
            self.n += 1
            return v, ("ar", self.n)

    AR = Arena()

    pe_n = [0]
    marks = []

    def pmark(name):
        marks.append((name, pe_n[0]))

    def mm(out, lhsT, rhs, start=True, stop=True):
        pe_n[0] += 1
        return lambda: nc.tensor.matmul(out, lhsT=lhsT, rhs=rhs, start=start, stop=stop)

    def tr(out, in_, ident):
        pe_n[0] += 1
        return lambda: nc.tensor.transpose(out, in_, ident)

    def act(out, in_, func, bias=None, scale=None, accum=None):
        kw = {}
        if bias is not None:
            kw["bias"] = bias
        if scale is not None:
            kw["scale"] = scale
        if accum is not None:
            kw["accum_out"] = accum
        return lambda: nc.scalar.activation(out=out, in_=in_, func=func, **kw)

    def vtt(out, in0, in1, op):
        return lambda: nc.vector.tensor_tensor(out=out, in0=in0, in1=in1, op=op)

    def vts(out, in0, s1, s2=None, op0=ALU.mult, op1=None):
        if op1 is None:
            return lambda: nc.vector.tensor_scalar(out=out, in0=in0, scalar1=s1, scalar2=None, op0=op0)
        return lambda: nc.vector.tensor_scalar(out=out, in0=in0, scalar1=s1, scalar2=s2, op0=op0, op1=op1)

    def vstt(out, in0, scalar, in1, op0, op1):
        return lambda: nc.vector.scalar_tensor_tensor(out=out, in0=in0, scalar=scalar, in1=in1, op0=op0, op1=op1)

    def vcp(out, in_):
        return lambda: nc.vector.tensor_copy(out=out, in_=in_)

    def vrec(out, in_):
        return lambda: nc.vector.reciprocal(out=out, in_=in_)

    def vset(out, v):
        return lambda: nc.vector.memset(out, v)

    def dma(out, in_, q="sp", nonc=False):
        e = nc.sync if q == "sp" else nc.gpsimd
        if nonc:
            return lambda: e.dma_start(out=out, in_=in_, allow_slow_non_contiguous=True)
        return lambda: e.dma_start(out=out, in_=in_)

    ch_misc = S.chan()
    ch_out = [S.chan() for _ in range(4)]
    ch_ring = [S.chan() for _ in range(NRING)]
    S.keep_chans = ch_ring
    ch_ld = [S.chan() for _ in range(6)]
    out_keys = []
    cnt = {"ring": 0, "out": 0, "ld": 0, "ps": 0}

    def wload(src, ncols, K=D):
        s = cnt["ring"] % NRING
        cnt["ring"] += 1
        S.add("pool", dma(ring[s][:, :, 0:ncols], src.rearrange("(j p) n -> p j n", p=128), q="pool"),
              writes=[("ring", s)], chan=ch_ring[s])
        return s

    def store(dst, src_ap, rkeys):
        c = ch_out[cnt["out"] % 4]
        cnt["out"] += 1
        k = ("hbm", cnt["out"])
        S.add("sp", dma(dst, src_ap), reads=rkeys, writes=[k], chan=c)
        return k

    def load(dst_ap, src, wkeys, rkeys=(), nonc=False):
        c = ch_ld[cnt["ld"] % 6]
        cnt["ld"] += 1
        S.add("sp", dma(dst_ap, src, nonc=nonc), reads=rkeys, writes=wkeys, chan=c)

    def bank():
        b = cnt["ps"] % 8
        cnt["ps"] += 1
        return b

    def dump(name, ap, rkeys, shape):
        if name in dbg:
            d = dout("dbg_" + name, shape, ap.dtype)
            dbg_out[name] = d
            out_keys.append(store(d, ap, rkeys))

    S.add("sp", dma(cst[:], cst_d), writes=["cst"], chan=ch_misc)
    S.add("dve", vcp(identb[:], C("ident")), reads=["cst"], writes=["identb"])
    S.add("dve", vcp(onesb[:], C("ones")), reads=["cst"], writes=["onesb"])
    S.barrier()

    def finish():
        S.barrier()
        st = S.emit()
        st["used_in"] = used_in
        st["marks"] = marks
        return nc, st, dbg_out

    def rstd_from_ss(ss, n, rk, wk):
        S.add("dve", vts(ss, ss, 1.0 / n, EPS, ALU.mult, ALU.add), reads=[rk], writes=[rk])
        S.add("act", act(ss, ss, AF.Sqrt), reads=[rk], writes=[rk])
        S.add("dve", vrec(ss, ss), reads=[rk], writes=[wk])

    def phase_mod(l):
        pmark('phase_mod')
        AR.reset()
        cTt, kc = AR.alloc([128, 32], F32)
        rowb = []
        for i in range(2):
            rowb.append(AR.alloc([128, 2048], F32))
        load(cTt, cT_d, [kc])
        S.add("act", act(sT[:], cTt, AF.Silu), reads=[kc], writes=["sT"])
        hk = []
        if "modin" in dbg:
            mi = din("modraw_in", [2, 6 * D])
            hk.append(store(modraw[l:l + 1, :], mi[l:l + 1, :], []))
        for ng in range(48 if "modin" not in dbg else 0):
            b = bank()
            for kh in range(2):
                s = cnt["ring"] % NRING
                cnt["ring"] += 1
                wv = ring[s][:].rearrange("p j n -> p (j n)").rearrange("p (a b) -> p a b", b=512)
                S.add("pool", dma(wv, w_mod[l][kh * 2048:(kh + 1) * 2048, ng * 512:(ng + 1) * 512].rearrange("(j p) n -> p j n", p=128), q="pool"),
                      writes=[("ring", s)], chan=ch_ring[s])
                S.add("pe", seq([mm(ps[b][0:1, :], sT[:, kh * 16 + j:kh * 16 + j + 1], wv[:, j, :], kh == 0 and j == 0, kh == 1 and j == 15) for j in range(16)]),
                      reads=["sT", ("ring", s)], writes=[("ps", b)])
            rb, rk = rowb[(ng // 4) % 2]
            c0 = (ng % 4) * 512
            S.add("act", act(rb[0:1, c0:c0 + 512], ps[b][0:1, :], AF.Copy), reads=[("ps", b)], writes=[rk])
            if ng % 4 == 3:
                g0 = (ng // 4) * 2048
                hk.append(store(modraw[l:l + 1, g0:g0 + 2048], rb[0:1, :], [rk]))
        stg, stgk = AR.alloc([128, 2, 128], F32)
        load(stg[0:96, :, :], modraw[l].rearrange("(a r p) -> r a p", a=2, p=128), [stgk], rkeys=hk)
        b = bank()
        S.add("pe", seq([tr(ps[b][:, a * 96:(a + 1) * 96], stg[0:96, a, :], C("ident")[0:96, 0:96]) for a in range(2)]),
              reads=[stgk], writes=[("ps", b)])
        S.add("act", act(modT[:], ps[b][:, 0:192], AF.Copy), reads=[("ps", b)], writes=[("modT", 0)])
        mk = [("modT", 0)]
        S.add("dve", vtt(modT[:], modT[:], C("bmodT%d" % l), ALU.add), reads=mk, writes=["modTf"])
        S.add("dve", vstt(gmod1T[:], modT[:, 32:64], 1.0, C("gpreT%d" % l), ALU.add, ALU.mult), reads=["modTf"], writes=["gmod1T"])
        S.add("dve", vstt(gmod2T[:], modT[:, 128:160], 1.0, C("gprefT%d" % l), ALU.add, ALU.mult), reads=["modTf"], writes=["gmod2T"])
        S.barrier()

    def norm_to_hT(xn, xk, tt, gT, shT):
        for g4 in range(8):
            b = bank()
            S.add("pe", seq([tr(ps[b][:, i * 128:(i + 1) * 128], xn[:, (g4 * 4 + i) * 128:(g4 * 4 + i + 1) * 128], C("ident")) for i in range(4)]),
                  reads=[xk], writes=[("ps", b)])
            for i in range(4):
                j = g4 * 4 + i
                o = bufA[:, j, tt * 128:(tt + 1) * 128]
                if False:
                    S.add("act", act(o, ps[b][:, i * 128:(i + 1) * 128], AF.Identity, bias=shT[:, j:j + 1], scale=gT[:, j:j + 1]),
                          reads=[("ps", b)], writes=[("hT", tt)])
                else:
                    S.add("dve", vts(o, ps[b][:, i * 128:(i + 1) * 128], gT[:, j:j + 1], shT[:, j:j + 1], ALU.mult, ALU.add),
                          reads=[("ps", b)], writes=[("hT", tt)])

    def sumsq(xt, xk, junk, jk, ss, sk):
        S.add("dve", vset(ss, 0.0), writes=[sk])
        S.add("act", act(junk, xt, AF.Square, accum=ss), reads=[xk, sk], writes=[jk, sk])

    def phase_n1(l, x_src):
        pmark('phase_n1')
        AR.reset()
        xts = [AR.alloc([128, D], F32) for _ in range(2)]
        junk, jk = AR.alloc([128, D], BF16)
        for tt in range(NT):
            xt, xk = xts[tt % 2]
            load(xt, x_src[tt * 128:(tt + 1) * 128, :], [xk])
            ss = small[:, tt:tt + 1]
            sk = ("ss", tt)
            sumsq(xt, xk, junk, jk, ss, sk)
            rstd_from_ss(ss, D, sk, sk)
            if "n1_nonorm" in dbg:
                continue
            S.add("dve", vts(xt, xt, ss), reads=[xk, sk], writes=[xk])
            if "n1_notr" in dbg:
                continue
            norm_to_hT(xt, xk, tt, gmod1T, modT[:, 0:32])
        S.barrier()

    def proj_fm(s, c0, b0, b1, extra_reads=()):
        for th, b in ((0, b0), (1, b1)):
            S.add("pe", seq([mm(ps[b][:, :], ring[s][:, j, c0:c0 + 128], bufA[:, j, th * 512:(th + 1) * 512], j == 0, j == NJ - 1) for j in range(NJ)]),
                  reads=[("ring", s), "hTall"] + list(extra_reads), writes=[("ps", b)])

    def proj_tm(s, tt, n, b, col0=0):
        S.add("pe", seq([mm(ps[b][:, col0:col0 + n], bufA[:, j, tt * 128:(tt + 1) * 128], ring[s][:, j, 0:n], j == 0, j == NJ - 1) for j in range(NJ)]),
              reads=[("ring", s), "hTall"], writes=[("ps", b)])

    def rope(src32, sk, R, cosT, sinT, tk, out_bf, ok, t1, t1k, b0, b1):
        for th, b in ((0, b0), (1, b1)):
            sl = slice(th * 512, (th + 1) * 512)
            S.add("pe", mm(ps[b][:, :], R, src32[:, sl]), reads=[sk], writes=[("ps", b)])
            S.add("dve", vtt(t1[:, sl], ps[b][:, :], sinT[:, sl], ALU.mult), reads=[("ps", b), tk], writes=[t1k])
        S.add("dve", vtt(src32, src32, cosT, ALU.mult), reads=[sk, tk], writes=[sk])
        if isinstance(out_bf, list):
            for o_, prt in out_bf:
                S.add("dve", vtt(o_[prt, :], src32[prt, :], t1[prt, :], ALU.add), reads=[sk, t1k], writes=[ok])
        else:
            S.add("dve", vtt(out_bf, src32, t1, ALU.add), reads=[sk, t1k], writes=[ok])

    def cache_kT(src, dstT, dk, stg, stgk, wd=128):
        load(stg, src.rearrange("(a p) d -> p a d", p=128), [stgk])
        b = bank()
        S.add("pe", seq([tr(ps[b][:, a * 128:(a + 1) * 128], stg[:, a, :], C("ident")) for a in range(2)]),
              reads=[stgk], writes=[("ps", b)])
        S.add("act", act(dstT, ps[b][:, 0:256], AF.Copy), reads=[("ps", b)], writes=[dk])

    def lam_compute(l, lam, lamk):
        li = 0.8 - 0.6 * math.exp(-0.3 * l)
        t, tk = AR.alloc([128, 64], F32)
        e1 = small[:, 16:17]; e2 = small[:, 17:18]
        S.add("dve", vtt(t, C("lq1_%d" % l), C("lk1_%d" % l), ALU.mult), writes=[tk])
        S.add("dve", lambda: nc.vector.reduce_sum(out=e1, in_=t, axis=mybir.AxisListType.X), reads=[tk], writes=["e1"])
        S.add("dve", vtt(t, C("lq2_%d" % l), C("lk2_%d" % l), ALU.mult), reads=["e1"], writes=[tk])
        S.add("dve", lambda: nc.vector.reduce_sum(out=e2, in_=t, axis=mybir.AxisListType.X), reads=[tk], writes=["e2"])
        S.add("act", act(e1, e1, AF.Exp), reads=["e1"], writes=["e1"])
        S.add("act", act(e2, e2, AF.Exp), reads=["e2"], writes=["e2"])
        S.add("dve", vtt(lam, e1, e2, ALU.subtract), reads=["e1", "e2"], writes=[lamk])
        S.add("dve", vts(lam, lam, li, None, ALU.add), reads=[lamk], writes=[lamk])
        return li

    def mix_store(chunk, src_bf, rk):
        out_keys.append(store(mixT_h[chunk * 128:(chunk + 1) * 128, :], src_bf, [rk]))

    def attn_core(nmaps, kparts, qT, qk, kT, kk, vb, vk, vcol, nkt, scale, pT, oacc, dacc, sbanks, maskcol):
        it = 0
        for qh in range(2):
            for kt in range(nkt):
                for m in range(nmaps):
                    pr = kparts[m]
                    sb = sbanks[it % len(sbanks)]
                    p, pk = pT[it % len(pT)]
                    it += 1
                    S.add("pe", mm(ps[sb][:, :], kT[pr, kt * 128:(kt + 1) * 128], qT[pr, qh * 512:(qh + 1) * 512]),
                          reads=[qk, kk], writes=[("ps", sb)])
                    for qb in range(2):
                        mc = maskcol(kt, qh * 2 + qb)
                        S.add("act", act(p[:, qb * 256:(qb + 1) * 256], ps[sb][:, qb * 256:(qb + 1) * 256], AF.Exp, bias=mc, scale=scale),
                              reads=[("ps", sb)], writes=[pk])
                    ob = oacc[m][qh]; db = dacc[m][qh]
                    S.add("pe", mm(ps[ob][:, :], vb[:, kt, vcol], p, kt == 0, kt == nkt - 1), reads=[pk, vk], writes=[("ps", ob)])
                    S.add("pe", mm(ps[db][:, :], onesb[:], p, kt == 0, kt == nkt - 1), reads=[pk, "onesb"], writes=[("ps", db)])

    def phase_mixA(l):
        pmark('phase_mixA')
        AR.reset()
        cosT, ck = AR.alloc([128, T], F32); sinT, _ = AR.alloc([128, T], F32)
        load(cosT, tabs_d[0], [ck]); load(sinT, tabs_d[1], [ck])
        lam, lamk = small[:, 18:19], "lam"
        li = lam_compute(l, lam, lamk)
        nlam, nlk = small[:, 19:20], "nlam"
        S.add("dve", vts(nlam, lam, -1.0), reads=[lamk], writes=[nlk])
        q32, q32k = AR.alloc([128, T], F32); k32, k32k = AR.alloc([128, T], F32)
        t1, t1k = AR.alloc([128, T], F32)
        qTbs = [AR.alloc([128, 2, T], BF16) for _ in range(2)]
        kTas = [AR.alloc([128, 1280], BF16) for _ in range(2)]
        for qTb, qTk in qTbs:
            S.add("dve", vset(qTb[64:128, 0, :], 0.0), writes=[qTk])
            S.add("dve", vset(qTb[0:64, 1, :], 0.0), writes=[qTk])
        vbs = [AR.alloc([128, 10, 256], BF16) for _ in range(2)]
        vst = [AR.alloc([128, 256], F32) for _ in range(2)]
        kst = [AR.alloc([128, 256], F32) for _ in range(2)]
        cks, cksk = AR.alloc([128, 2, 128], F32); cvs, cvsk = AR.alloc([128, 2, 256], F32)
        pT = [AR.alloc([128, 512], BF16) for _ in range(4)]
        e1, e1k = AR.alloc([128, 512], F32); e2, e2k = AR.alloc([128, 512], F32); e3, e3k = AR.alloc([128, 512], F32)
        ob, obk = AR.alloc([128, T], BF16)
        slots = {}

        def pair_prep(hp):
            sq = wload(w_in[l][:, OQA + hp * 256:OQA + (hp + 1) * 256], 256)
            sk_ = wload(w_in[l][:, OKA + hp * 256:OKA + (hp + 1) * 256], 256)
            sv = wload(w_in[l][:, OVA + hp * 256:OVA + (hp + 1) * 256], 256)
            slots[hp] = (sq, sk_)
            vb, vk = vbs[hp % 2]
            for tt in range(NT):
                b = bank()
                proj_tm(sk_, tt, 256, b)
                st, stk = kst[tt % 2]
                S.add("act", act(st, ps[b][:, 0:256], AF.Copy), reads=[("ps", b)], writes=[stk])
                out_keys.append(store(nak[l, tt * 128:(tt + 1) * 128, hp * 256:(hp + 1) * 256], st, [stk]))
                b = bank()
                proj_tm(sv, tt, 256, b)
                st, stk = vst[tt % 2]
                S.add("act", act(st, ps[b][:, 0:256], AF.Copy), reads=[("ps", b)], writes=[stk])
                S.add("dve", vcp(vb[:, 2 + tt, :], st), reads=[stk], writes=[vk])
                out_keys.append(store(nav[l, tt * 128:(tt + 1) * 128, hp * 256:(hp + 1) * 256], st, [stk]))
            load(cvs, cav[l][:, hp * 256:(hp + 1) * 256].rearrange("(a p) d -> p a d", p=128), [cvsk])
            S.add("dve", vcp(vb[:, 0:2, :], cvs), reads=[cvsk], writes=[vk])

        def prep(h):
            hp, hh = h // 2, h % 2
            sq, sk_ = slots[hp]
            c0 = hh * 128
            qTb, qTk = qTbs[h % 2]
            kTa, kTk = kTas[h % 2]
            proj_fm(sq, c0, 0, 1)
            S.add("act", act(q32[:, 0:512], ps[0][:, :], AF.Copy), reads=[("ps", 0)], writes=[q32k])
            S.add("act", act(q32[:, 512:1024], ps[1][:, :], AF.Copy), reads=[("ps", 1)], writes=[q32k])
            rope(q32, q32k, C("RA"), cosT, sinT, ck, [(qTb[:, 0, :], slice(0, 64)), (qTb[:, 1, :], slice(64, 128))], qTk, t1, t1k, 2, 3)
            proj_fm(sk_, c0, 0, 1)
            S.add("act", act(k32[:, 0:512], ps[0][:, :], AF.Copy), reads=[("ps", 0)], writes=[k32k])
            S.add("act", act(k32[:, 512:1024], ps[1][:, :], AF.Copy), reads=[("ps", 1)], writes=[k32k])
            rope(k32, k32k, C("RA"), cosT, sinT, ck, kTa[:, 256:1280], kTk, t1, t1k, 2, 3)
            cache_kT(cak[l][:, h * 128:(h + 1) * 128], kTa[:, 0:256], kTk, cks, cksk)

        pair_prep(0)
        prep(0)
        for h in range(8):
            if h + 1 < 8:
                if (h + 1) % 2 == 0:
                    pair_prep((h + 1) // 2)
                prep(h + 1)
            qTb, qTk = qTbs[h % 2]
            kTa, kTk = kTas[h % 2]
            vb, vk = vbs[(h // 2) % 2]
            _attnA(l, h, (h % 2) * 128, qTb, qTk, kTa, kTk, vb, vk, pT, e1, e1k, e2, e2k, e3, e3k, ob, obk, nlam, nlk, li)
        S.barrier()

    def _attnA(l, h, c0, qTb, qTk, kTa, kTk, vb, vk, pT, e1, e1k, e2, e2k, e3, e3k, ob, obk, nlam, nlk, li):
        LA = 2
        it0 = [0]
        for qh in range(2):
            qs = slice(qh * 512, (qh + 1) * 512)
            items = [(kt, m) for kt in range(10) for m in range(2)]
            base = it0[0]
            it0[0] += len(items)

            def score(i, qh=qh, qs=qs, base=base):
                kt, m = items[i]
                sb = 4 + (base + i) % 4
                p, pk = pT[(base + i) % 4]
                S.add("pe", mm(ps[sb][:, :], kTa[:, kt * 128:(kt + 1) * 128], qTb[:, m, qs]), reads=[qTk, kTk], writes=[("ps", sb)])
                for qb in range(2):
                    o = kt * 4 + qh * 2 + qb
                    S.add("act", act(p[:, qb * 256:(qb + 1) * 256], ps[sb][:, qb * 256:(qb + 1) * 256], AF.Exp, bias=C("maskAB", o, o + 1), scale=0.125),
                          reads=[("ps", sb)], writes=[pk])

            def pv(i, base=base):
                kt, m = items[i]
                p, pk = pT[(base + i) % 4]
                S.add("pe", mm(ps[2 * m][:, :], vb[:, kt, c0:c0 + 128], p, kt == 0, kt == 9), reads=[pk, vk], writes=[("ps", 2 * m)])
                S.add("pe", mm(ps[2 * m + 1][:, :], onesb[:], p, kt == 0, kt == 9), reads=[pk, "onesb"], writes=[("ps", 2 * m + 1)])

            for i in range(len(items) + LA):
                if i < len(items):
                    score(i)
                if i - LA >= 0:
                    pv(i - LA)
            S.add("dve", vrec(e1, ps[1][:, :]), reads=[("ps", 1)], writes=[e1k])
            S.add("dve", vtt(e1, ps[0][:, :], e1, ALU.mult), reads=[("ps", 0), e1k], writes=[e1k])
            S.add("dve", vrec(e2, ps[3][:, :]), reads=[("ps", 3)], writes=[e2k])
            S.add("dve", vtt(e2, ps[2][:, :], e2, ALU.mult), reads=[("ps", 2), e2k], writes=[e2k])
            S.add("dve", vstt(e1, e2, nlam, e1, ALU.mult, ALU.add), reads=[e1k, e2k, nlk], writes=[e1k])
            S.add("dve", vtt(e3, e1, e1, ALU.mult), reads=[e1k], writes=[e3k])
            S.add("pe", mm(ps[0][:, :], C("ones"), e3), reads=[e3k], writes=[("ps", 0)])
            S.add("dve", vts(e3, ps[0][:, :], 1.0 / 128, EPS, ALU.mult, ALU.add), reads=[("ps", 0)], writes=[e3k])
            S.add("act", act(e3, e3, AF.Sqrt), reads=[e3k], writes=[e3k])
            S.add("dve", vrec(e3, e3), reads=[e3k], writes=[e3k])
            S.add("dve", vtt(e1, e1, e3, ALU.mult), reads=[e1k, e3k], writes=[e1k])
            S.add("dve", vts(ob[:, qs], e1, C("gsubT%d" % l), 1.0 - li, ALU.mult, ALU.mult), reads=[e1k], writes=[obk])
        mix_store(h, ob, obk)

    def bcast_row(tensor_ap_1d):
        n = tensor_ap_1d.shape[0]
        return bass.AP(tensor_ap_1d.tensor, tensor_ap_1d.offset, [[0, 128], [1, n]])

    def qk_norm_fm(x32, xk, sq, sqk, gT):
        S.add("dve", vtt(sq, x32, x32, ALU.mult), reads=[xk], writes=[sqk])
        for th in range(2):
            sl = slice(th * 512, (th + 1) * 512)
            b = bank()
            S.add("pe", mm(ps[b][:, :], C("ones"), sq[:, sl]), reads=[sqk], writes=[("ps", b)])
            S.add("dve", vts(sq[:, sl], ps[b][:, :], 1.0 / 128, EPS, ALU.mult, ALU.add), reads=[("ps", b)], writes=[sqk])
        S.add("act", act(sq, sq, AF.Sqrt), reads=[sqk], writes=[sqk])
        S.add("dve", vrec(sq, sq), reads=[sqk], writes=[sqk])
        S.add("dve", vstt(x32, x32, gT, sq, ALU.mult, ALU.mult), reads=[xk, sqk], writes=[xk])

    def fm_to_32(dst, dk, b0=0, b1=1):
        S.add("act", act(dst[:, 0:512], ps[b0][:, :], AF.Copy), reads=[("ps", b0)], writes=[dk])
        S.add("act", act(dst[:, 512:1024], ps[b1][:, :], AF.Copy), reads=[("ps", b1)], writes=[dk])

    def attn_single(qTb, qTk, kTa, kTk, vb, vk, vcols, pT, scale, e1, e1k, ob, obk):
        LA = 2
        base0 = [0]
        for qh in range(2):
            qs = slice(qh * 512, (qh + 1) * 512)
            base = base0[0]
            base0[0] += 10

            def score(kt, qh=qh, qs=qs, base=base):
                sb = 4 + (base + kt) % 4
                p, pk = pT[(base + kt) % 4]
                S.add("pe", mm(ps[sb][:, :], kTa[:, kt * 128:(kt + 1) * 128], qTb[:, qs]), reads=[qTk, kTk], writes=[("ps", sb)])
                for qb in range(2):
                    o = kt * 4 + qh * 2 + qb
                    S.add("act", act(p[:, qb * 256:(qb + 1) * 256], ps[sb][:, qb * 256:(qb + 1) * 256], AF.Exp, bias=C("maskAB", o, o + 1), scale=scale),
                          reads=[("ps", sb)], writes=[pk])

            def pv(kt, base=base):
                p, pk = pT[(base + kt) % 4]
                S.add("pe", mm(ps[0][:, :], vb[:, kt, vcols], p, kt == 0, kt == 9), reads=[pk, vk], writes=[("ps", 0)])
                S.add("pe", mm(ps[1][:, :], onesb[:], p, kt == 0, kt == 9), reads=[pk, "onesb"], writes=[("ps", 1)])

            for i in range(10 + LA):
                if i < 10:
                    score(i)
                if i - LA >= 0:
                    pv(i - LA)
            S.add("dve", vrec(e1, ps[1][:, :]), reads=[("ps", 1)], writes=[e1k])
            S.add("dve", vtt(ob[:, qs], ps[0][:, :], e1, ALU.mult), reads=[("ps", 0), e1k], writes=[obk])

    def phase_mixB(l):
        pmark('phase_mixB')
        AR.reset()
        cosT, ck = AR.alloc([128, T], F32); sinT, _ = AR.alloc([128, T], F32)
        load(cosT, tabs_d[2], [ck]); load(sinT, tabs_d[3], [ck])
        q32, q32k = AR.alloc([128, T], F32); k32, k32k = AR.alloc([128, T], F32)
        t1, t1k = AR.alloc([128, T], F32); sq, sqk = AR.alloc([128, T], F32)
        qTb, qTk = AR.alloc([128, T], BF16)
        kTa = [AR.alloc([128, 1280], BF16) for _ in range(2)]
        vb, vk = AR.alloc([128, 10, 256], BF16)
        vst = [AR.alloc([128, 256], F32) for _ in range(2)]
        kst = [AR.alloc([128, 256], F32) for _ in range(2)]
        tm, tmk = AR.alloc([128, 256], F32)
        cks, cksk = AR.alloc([128, 2, 128], F32); cvs, cvsk = AR.alloc([128, 2, 256], F32)
        pT = [AR.alloc([128, 512], BF16) for _ in range(4)]
        e1, e1k = AR.alloc([128, 512], F32)
        ob, obk = AR.alloc([128, T], BF16)
        sk_ = wload(w_in[l][:, OKB:OKB + 256], 256)
        sv = wload(w_in[l][:, OVB:OVB + 256], 256)
        gk2 = bass.AP(cst, CL["gk_bc%d" % l][0], [[NCST, 128], [0, 2], [1, 128]])
        for tt in range(NT):
            b = bank()
            proj_tm(sk_, tt, 256, b)
            st, stk = kst[tt % 2]
            ss2 = small[:, 24 + 2 * (tt % 2):26 + 2 * (tt % 2)]
            ssk = ("ss2", tt % 2)
            S.add("dve", vcp(st, ps[b][:, 0:256]), reads=[("ps", b)], writes=[stk])
            S.add("dve", vtt(tm, st, st, ALU.mult), reads=[stk], writes=[tmk])
            S.add("dve", (lambda o=ss2, i=tm: nc.vector.reduce_sum(out=o, in_=i.rearrange("p (g d) -> p g d", d=128), axis=mybir.AxisListType.X)),
                  reads=[tmk], writes=[ssk])
            rstd_from_ss(ss2, 128, ssk, ssk)
            st3 = st.rearrange("p (g d) -> p g d", d=128)
            rb = bass.AP(small, 24 + 2 * (tt % 2), [[64, 128], [1, 2], [0, 128]])
            S.add("dve", vtt(st3, st3, rb, ALU.mult), reads=[stk, ssk], writes=[stk])
            S.add("dve", vtt(st3, st3, gk2, ALU.mult), reads=[stk], writes=[stk])
            out_keys.append(store(nbk[l, tt * 128:(tt + 1) * 128, :], st, [stk]))
            b = bank()
            proj_tm(sv, tt, 256, b)
            st, stk = vst[tt % 2]
            S.add("act", act(st, ps[b][:, 0:256], AF.Copy), reads=[("ps", b)], writes=[stk])
            S.add("dve", vcp(vb[:, 2 + tt, :], st), reads=[stk], writes=[vk])
            out_keys.append(store(nbv[l, tt * 128:(tt + 1) * 128, :], st, [stk]))
        load(cvs, cbv[l].rearrange("(a p) d -> p a d", p=128), [cvsk])
        S.add("dve", vcp(vb[:, 0:2, :], cvs), reads=[cvsk], writes=[vk])
        for g in range(2):
            kT_, kTk = kTa[g]
            proj_fm(sk_, g * 128, 0, 1)
            fm_to_32(k32, k32k)
            qk_norm_fm(k32, k32k, sq, sqk, C("gkT%d" % l))
            rope(k32, k32k, C("RBC"), cosT, sinT, ck, kT_[:, 256:1280], kTk, t1, t1k, 2, 3)
            cache_kT(cbk[l][:, g * 128:(g + 1) * 128], kT_[:, 0:256], kTk, cks, cksk)
        qTb2 = [(qTb, qTk), AR.alloc([128, T], BF16)]
        qslot = {}

        def prepB(h):
            if h % 2 == 0:
                qslot[h // 2] = wload(w_in[l][:, OQB + (h // 2) * 256:OQB + (h // 2 + 1) * 256], 256)
            q_, qk_ = qTb2[h % 2]
            proj_fm(qslot[h // 2], (h % 2) * 128, 0, 1)
            fm_to_32(q32, q32k)
            qk_norm_fm(q32, q32k, sq, sqk, C("gqT%d" % l))
            rope(q32, q32k, C("RBC"), cosT, sinT, ck, q_, qk_, t1, t1k, 2, 3)

        prepB(0)
        for h in range(8):
            if h + 1 < 8:
                prepB(h + 1)
            g = h // 4
            q_, qk_ = qTb2[h % 2]
            attn_single(q_, qk_, kTa[g][0], kTa[g][1], vb, vk, slice(g * 128, (g + 1) * 128), pT, 128 ** -0.5, e1, e1k, ob, obk)
            mix_store(8 + h, ob, obk)
        S.barrier()

    def phase_mixC(l):
        pmark('phase_mixC')
        AR.reset()
        cosT, ck = AR.alloc([128, T], F32); sinT, _ = AR.alloc([128, T], F32)
        load(cosT, tabs_d[2], [ck]); load(sinT, tabs_d[3], [ck])
        cm, cmk = AR.alloc([128, 2048], F32)
        load(cm, cmask_d, [cmk])
        q32, q32k = AR.alloc([128, T], F32); k32, k32k = AR.alloc([128, T], F32)
        t1, t1k = AR.alloc([128, T], F32)
        qTb, qTk = AR.alloc([128, T], BF16)
        kTa = [AR.alloc([128, 1280], BF16) for _ in range(2)]
        vb, vk = AR.alloc([128, 10, 256], BF16)
        vst = [AR.alloc([128, 256], F32) for _ in range(2)]
        kst = [AR.alloc([128, 256], F32) for _ in range(2)]
        cks, cksk = AR.alloc([128, 2, 128], F32); cvs, cvsk = AR.alloc([128, 2, 256], F32)
        pT = [AR.alloc([128, 128], BF16) for _ in range(4)]
        pf = [AR.alloc([128, 128], F32) for _ in range(2)]
        e1, e1k = AR.alloc([128, 128], F32)
        ob, obk = AR.alloc([128, T], BF16)
        esk = small[:, 32:40]
        S.add("act", act(esk, C("sink_bc%d" % l), AF.Exp), writes=["esk"])
        sk_ = wload(w_in[l][:, OKC:OKC + 256], 256)
        sv = wload(w_in[l][:, OVC:OVC + 256], 256)
        for tt in range(NT):
            b = bank()
            proj_tm(sk_, tt, 256, b)
            st, stk = kst[tt % 2]
            S.add("act", act(st, ps[b][:, 0:256], AF.Copy), reads=[("ps", b)], writes=[stk])
            out_keys.append(store(nck[l, tt * 128:(tt + 1) * 128, :], st, [stk]))
            b = bank()
            proj_tm(sv, tt, 256, b)
            st, stk = vst[tt % 2]
            S.add("act", act(st, ps[b][:, 0:256], AF.Copy), reads=[("ps", b)], writes=[stk])
            S.add("dve", vcp(vb[:, 2 + tt, :], st), reads=[stk], writes=[vk])
            out_keys.append(store(ncv[l, tt * 128:(tt + 1) * 128, :], st, [stk]))
        load(cvs, ccv[l].rearrange("(a p) d -> p a d", p=128), [cvsk])
        S.add("dve", vcp(vb[:, 0:2, :], cvs), reads=[cvsk], writes=[vk])
        for g in range(2):
            kT_, kTk = kTa[g]
            proj_fm(sk_, g * 128, 0, 1)
            fm_to_32(k32, k32k)
            rope(k32, k32k, C("RBC"), cosT, sinT, ck, kT_[:, 256:1280], kTk, t1, t1k, 2, 3)
            cache_kT(cck[l][:, g * 128:(g + 1) * 128], kT_[:, 0:256], kTk, cks, cksk)
        scale = 128 ** -0.5
        it = 0
        for qs_ in range(4):
            sq_ = wload(w_in[l][:, OQC + qs_ * 256:OQC + (qs_ + 1) * 256], 256)
            for hh in range(2):
                h = qs_ * 2 + hh
                g = h // 4
                kT_, kTk = kTa[g]
                proj_fm(sq_, hh * 128, 0, 1)
                fm_to_32(q32, q32k)
                rope(q32, q32k, C("RBC"), cosT, sinT, ck, qTb, qTk, t1, t1k, 2, 3)
                items = []
                for n in range(8):
                    tiles = [(0, "ctx"), (1, "ctx")]
                    if n > 0:
                        tiles.append((2 + n - 1, "prev"))
                    tiles.append((2 + n, "diag"))
                    if n < 7:
                        tiles.append((2 + n + 1, "next"))
                    for ti, (kt, kind) in enumerate(tiles):
                        items.append((n, kt, kind, ti == 0, ti == len(tiles) - 1))
                base = it
                it += len(items)

                def score(i, base=base, kT_=kT_, kTk=kTk):
                    n, kt, kind, first, last = items[i]
                    qs = slice(n * 128, (n + 1) * 128)
                    sb = 4 + (base + i) % 4
                    p, pk = pT[(base + i) % 4]
                    S.add("pe", mm(ps[sb][:, 0:128], kT_[:, kt * 128:(kt + 1) * 128], qTb[:, qs]), reads=[qTk, kTk], writes=[("ps", sb)])
                    if kind == "ctx":
                        S.add("act", act(p, ps[sb][:, 0:128], AF.Exp, bias=C("maskC"), scale=scale), reads=[("ps", sb)], writes=[pk])
                    elif kind == "diag":
                        S.add("act", act(p, ps[sb][:, 0:128], AF.Exp, scale=scale), reads=[("ps", sb)], writes=[pk])
                    else:
                        f_, fk = pf[(base + i) % 2]
                        S.add("act", act(f_, ps[sb][:, 0:128], AF.Exp, scale=scale), reads=[("ps", sb)], writes=[fk])
                        side = 0 if kind == "prev" else 1
                        mo = (n * 2 + side) * 128
                        S.add("dve", vtt(p, f_, cm[:, mo:mo + 128], ALU.mult), reads=[fk, cmk], writes=[pk])

                def pv(i, base=base, g=g, h=h):
                    n, kt, kind, first, last = items[i]
                    qs = slice(n * 128, (n + 1) * 128)
                    ob_, db_ = (0, 1) if n % 2 == 0 else (2, 3)
                    p, pk = pT[(base + i) % 4]
                    S.add("pe", mm(ps[ob_][:, 0:128], vb[:, kt, g * 128:(g + 1) * 128], p, first, last), reads=[pk, vk], writes=[("ps", ob_)])
                    S.add("pe", mm(ps[db_][:, 0:128], onesb[:], p, first, last), reads=[pk, "onesb"], writes=[("ps", db_)])
                    if last:
                        S.add("dve", vts(e1, ps[db_][:, 0:128], esk[:, h:h + 1], None, ALU.add), reads=[("ps", db_), "esk"], writes=[e1k])
                        S.add("dve", vrec(e1, e1), reads=[e1k], writes=[e1k])
                        S.add("dve", vtt(ob[:, qs], ps[ob_][:, 0:128], e1, ALU.mult), reads=[("ps", ob_), e1k], writes=[obk])

                LA = 2
                for i in range(len(items) + LA):
                    if i < len(items):
                        score(i)
                    if i - LA >= 0:
                        pv(i - LA)
                mix_store(16 + h, ob, obk)
        S.barrier()

    dtall = A("dtall", [128, NT, 32], F32)
    dtA = A("dtA", [128, NT, 32], F32)
    abc = A("abc", [128, 32], F32)
    dtot = A("dtot", [128, 16], F32)
    chq = A("chq", [128, 4, 32], F32)
    Hst = A("Hst", [128, 1024], F32)
    zs_h = dscr("zs_h", [T, 1024]); yp_h = dscr("yp_h", [T, 1024])

    def bc64(t, off, rowlen, nh=16):
        return bass.AP(t, off, [[rowlen, 128], [1, nh], [0, 64]])

    def phase_mixD(l):
        pmark('phase_mixD')
        AR.reset()
        xsT, xsk = AR.alloc([128, 8, T], BF16)
        BmT, bmk = AR.alloc([128, 2, T], BF16); CmT, cmk_ = AR.alloc([128, 2, T], BF16)
        mark = AR.off
        u32, uk = AR.alloc([128, 4, 260], F32); acc, ak = AR.alloc([128, 4, 256], F32)
        zst = [AR.alloc([128, 256], F32) for _ in range(2)]
        S.add("dve", vset(u32[:, 0, 0:2], 0.0), writes=[uk])
        S.add("dve", vset(u32[:, 3, 258:260], 0.0), writes=[uk])
        for sl_ in range(6):
            s = wload(w_in[l][:, OXBC + sl_ * 256:OXBC + (sl_ + 1) * 256], 256)
            for hh in range(2):
                cc = sl_ * 2 + hh
                proj_fm(s, hh * 128, 0, 1)
                for th, b in ((0, 0), (1, 1)):
                    S.add("act", act(u32[:, 2 * th:2 * th + 2, 2:258], ps[b][:, :].rearrange("p (a t) -> p a t", t=256), AF.Copy),
                          reads=[("ps", b)], writes=[uk])
                S.add("dve", vts(u32[:, 1:4, 0:2], u32[:, 0:3, 256:258], C("flag")), reads=[uk], writes=[uk])
                S.add("dve", vts(u32[:, 0:3, 258:260], u32[:, 1:4, 2:4], C("flag")), reads=[uk], writes=[uk])
                wo = CL["convw%d" % l][0] + cc * 5
                S.add("dve", vts(acc, u32[:, :, 0:256], cst[:, wo:wo + 1]), reads=[uk], writes=[ak])
                for k in range(1, 5):
                    S.add("dve", vstt(acc, u32[:, :, k:k + 256], cst[:, wo + k:wo + k + 1], acc, ALU.mult, ALU.add), reads=[uk, ak], writes=[ak])
                bo = CL["convb%d" % l][0] + cc
                if cc < 8:
                    dst = xsT[:, cc, :]; dk = xsk
                elif cc < 10:
                    dst = BmT[:, cc - 8, :]; dk = bmk
                else:
                    dst = CmT[:, cc - 10, :]; dk = cmk_
                S.add("act", act(dst.rearrange("p (a t) -> p a t", t=256), acc, AF.Silu, bias=cst[:, bo:bo + 1]), reads=[ak], writes=[dk])
        s = wload(w_in[l][:, ODT:ODT + 32], 32)
        for tt in range(NT):
            b = bank()
            proj_tm(s, tt, 32, b)
            S.add("dve", vtt(dtall[:, tt, :], ps[b][:, 0:32], C("dtb%d" % l), ALU.add), reads=[("ps", b)], writes=["dtall"])
        dflat = dtall[:].rearrange("p a b -> p (a b)")
        S.add("act", act(dflat, dflat, AF.Exp), reads=["dtall"], writes=["dtall"])
        S.add("act", act(dflat, dflat, AF.Ln, bias=1.0), reads=["dtall"], writes=["dtall"])
        S.add("act", act(abc[:], C("alog%d" % l), AF.Exp), writes=["abc"])
        S.add("dve", vts(abc[:], abc[:], -1.0), reads=["abc"], writes=["abc"])
        abc3 = bass.AP(abc, 0, [[32, 128], [0, NT], [1, 32]])
        S.add("dve", vtt(dtA[:], dtall[:], abc3, ALU.mult), reads=["dtall", "abc"], writes=["dtA"])
        S.add("dve", vtt(dtot[:], C("dsk%d" % l, 0, 16), C("dsk%d" % l, 16, 32), ALU.add), writes=["dtot"])
        zkeys = []
        for zi in range(4):
            s = wload(w_in[l][:, OZ + zi * 256:OZ + (zi + 1) * 256], 256)
            for tt in range(NT):
                b = bank()
                proj_tm(s, tt, 256, b)
                st, stk = zst[tt % 2]
                S.add("act", act(st, ps[b][:, 0:256], AF.Silu), reads=[("ps", b)], writes=[stk])
                zkeys.append(store(zs_h[tt * 128:(tt + 1) * 128, zi * 256:(zi + 1) * 256], st, [stk]))
        S.barrier()
        pmark('mixD_chunks')
        AR.off = mark
        xs_t, xstk = AR.alloc([128, 1024], BF16)
        xdt, xdtk = AR.alloc([128, 1024], BF16); xdw, xdwk = AR.alloc([128, 1024], BF16)
        bmt, bmtk = AR.alloc([128, 256], BF16)
        mcb = [AR.alloc([128, 128], F32) for _ in range(2)]
        lt = [AR.alloc([128, 128], F32) for _ in range(4)]
        ex = [AR.alloc([128, 128], F32) for _ in range(4)]
        MT = [AR.alloc([128, 128], BF16) for _ in range(4)]
        yf, yfk = AR.alloc([128, 1024], F32); y2, y2k = AR.alloc([128, 1024], F32)
        zl, zlk = AR.alloc([128, 1024], F32)
        Hb, Hbk = AR.alloc([128, 1024], BF16)
        hst, hstk = AR.alloc([128, 8, 128], F32)
        odb, odk = AR.alloc([128, 1024], BF16); odT, odTk = AR.alloc([128, 8, 128], BF16)
        gss, gsk = AR.alloc([128, 1024], F32)
        load(gss, gssm_d[l], [gsk])

        def load_H(d):
            load(hst, h0_d[l, d].rearrange("(a p) n -> p a n", p=128), [hstk])
            for half in range(2):
                b = 6 + half
                S.add("pe", seq([tr(ps[b][:, i * 128:(i + 1) * 128], hst[:, half * 4 + i, :], C("ident")) for i in range(4)]),
                      reads=[hstk], writes=[("ps", b)])
                S.add("act", act(Hst[:, half * 512:(half + 1) * 512], ps[b][:, :], AF.Copy), reads=[("ps", b)], writes=["Hst"])

        def store_H(sq_, d):
            for half in range(2):
                b = 6 + half
                S.add("pe", seq([tr(ps[b][:, i * 128:(i + 1) * 128], Hst[:, (half * 4 + i) * 128:(half * 4 + i + 1) * 128], C("ident")) for i in range(4)]),
                      reads=["Hst"], writes=[("ps", b)])
                S.add("act", act(hst[:, half * 4:half * 4 + 4, :], ps[b][:, :].rearrange("p (a n) -> p a n", n=128), AF.Copy),
                      reads=[("ps", b)], writes=[hstk])
            out_keys.append(store(nssm[l, sq_, d].rearrange("(a p) n -> p a n", p=128), hst, [hstk]))

        def chunk(c, d):
            cs = slice(c * 128, (c + 1) * 128)
            do = d * 16
            psb = ps[7][:, :].bitcast(BF16)
            S.add("pe", seq([tr(psb[:, i * 128:(i + 1) * 128], xsT[:, i, cs], identb[:]) for i in range(8)]), reads=[xsk], writes=[("ps", 7)])
            S.add("act", act(xs_t, psb[:, 0:1024], AF.Copy), reads=[("ps", 7)], writes=[xstk])
            S.add("pe", seq([tr(psb[:, i * 128:(i + 1) * 128], BmT[:, i, cs], identb[:]) for i in range(2)]), reads=[bmk, xstk], writes=[("ps", 7)])
            S.add("act", act(bmt, psb[:, 0:256], AF.Copy), reads=[("ps", 7)], writes=[bmtk])
            triX = C("triU") if d == 0 else C("triL")
            S.add("pe", mm(ps[6][:, 0:16], triX, dtA[:, c, do:do + 16]), reads=["dtA", bmtk], writes=[("ps", 6)])
            S.add("pe", mm(ps[6][:, 16:32], C("ones"), dtA[:, c, do:do + 16]), reads=["dtA"], writes=[("ps", 6)])
            S.add("dve", vcp(chq[:, 0, 0:32], ps[6][:, 0:32]), reads=[("ps", 6)], writes=["chq"])
            S.add("act", act(chq[:, 1, 0:16], chq[:, 0, 0:16], AF.Exp), reads=["chq"], writes=["chq"])
            S.add("dve", vtt(chq[:, 2, 0:16], chq[:, 0, 16:32], chq[:, 0, 0:16], ALU.subtract), reads=["chq"], writes=["chq"])
            S.add("act", act(chq[:, 2, 0:16], chq[:, 2, 0:16], AF.Exp), reads=["chq"], writes=["chq"])
            S.add("act", act(chq[:, 3, 0:16], chq[:, 0, 16:32], AF.Exp), reads=["chq"], writes=["chq"])
            x3 = xs_t.rearrange("p (h e) -> p h e", e=64)
            S.add("dve", vtt(xdt.rearrange("p (h e) -> p h e", e=64), x3, bc64(dtall, c * 32 + do, NT * 32), ALU.mult), reads=[xstk, "dtall"], writes=[xdtk])
            S.add("dve", vtt(xdw.rearrange("p (h e) -> p h e", e=64), xdt.rearrange("p (h e) -> p h e", e=64), bc64(chq, 2 * 32, 4 * 32), ALU.mult),
                  reads=[xdtk, "chq"], writes=[xdwk])
            trimask = C("triU") if d == 0 else C("triL")
            for g in range(2):
                S.add("pe", mm(ps[6][:, 128 + g * 128:256 + g * 128], BmT[:, g, cs], CmT[:, g, cs]), reads=[bmk, cmk_, "chq"], writes=[("ps", 6)])
                m_, mk_ = mcb[g]
                S.add("dve", vtt(m_, ps[6][:, 128 + g * 128:256 + g * 128], trimask, ALU.mult), reads=[("ps", 6)], writes=[mk_])
            sX = C("sL") if d == 0 else C("sU")
            def seg_part(h):
                lt_, ltk = lt[h % 4]; ex_, exk = ex[h % 4]
                S.add("dve", vts(lt_, sX, dtA[:, c, do + h:do + h + 1]), reads=["dtA"], writes=[ltk])
                sb = 4 + (h % 2)
                sc0 = ((h // 2) % 4) * 128
                S.add("pe", mm(ps[sb][:, sc0:sc0 + 128], lt_, triX), reads=[ltk], writes=[("ps", sb)])
                S.add("act", act(ex_, ps[sb][:, sc0:sc0 + 128], AF.Exp), reads=[("ps", sb)], writes=[exk])

            def yd_part(h):
                g = h // 8
                ex_, exk = ex[h % 4]; M_, Mk = MT[h % 4]
                S.add("dve", vtt(M_, ex_, mcb[g][0], ALU.mult), reads=[exk, mcb[g][1]], writes=[Mk])
                S.add("pe", mm(ps[h // 8][:, (h % 8) * 64:(h % 8 + 1) * 64], M_, xdt[:, h * 64:(h + 1) * 64]), reads=[Mk, xdtk], writes=[("ps", h // 8)])

            LA = 2
            for i in range(16 + LA):
                if i < 16:
                    seg_part(i)
                if i - LA >= 0:
                    yd_part(i - LA)
            for g in range(2):
                S.add("pe", mm(ps[2 + g][:, :], CmT[:, g, cs], Hb[:, g * 512:(g + 1) * 512]), reads=[cmk_, Hbk], writes=[("ps", 2 + g)])
            for g in range(2):
                sl = slice(g * 512, (g + 1) * 512)
                S.add("dve", vtt(yf[:, sl].rearrange("p (h e) -> p h e", e=64), ps[2 + g][:, :].rearrange("p (h e) -> p h e", e=64),
                                 bc64(chq, 32 + g * 8, 4 * 32, nh=8), ALU.mult), reads=[("ps", 2 + g), "chq"], writes=[yfk])
                S.add("dve", vtt(yf[:, sl], yf[:, sl], ps[g][:, :], ALU.add), reads=[("ps", g), yfk], writes=[yfk])
            for g in range(2):
                sl = slice(g * 512, (g + 1) * 512)
                S.add("pe", mm(ps[2 + g][:, :], bmt[:, g * 128:(g + 1) * 128], xdw[:, sl]), reads=[bmtk, xdwk, yfk], writes=[("ps", 2 + g)])
                S.add("dve", vtt(Hst[:, sl].rearrange("p (h e) -> p h e", e=64), Hst[:, sl].rearrange("p (h e) -> p h e", e=64),
                                 bc64(chq, 3 * 32 + g * 8, 4 * 32, nh=8), ALU.mult), reads=["Hst", "chq", Hbk], writes=["Hst"])
                S.add("dve", vtt(Hst[:, sl], Hst[:, sl], ps[2 + g][:, :], ALU.add), reads=["Hst", ("ps", 2 + g)], writes=["Hst"])

        def set_Hb():
            S.add("act", act(Hb, Hst[:], AF.Copy), reads=["Hst"], writes=[Hbk])

        load_H(0)
        ypk = {}
        for c in range(NT):
            if c >= 2 and c % 2 == 0:
                S.add("dve", vts(Hst[:], Hst[:], C("flag")), reads=["Hst"], writes=["Hst"])
            set_Hb()
            chunk(c, 0)
            ypk[c] = store(yp_h[c * 128:(c + 1) * 128, :], yf, [yfk])
            if c % 2 == 1:
                store_H(c // 2, 0)
        load_H(1)
        for c in range(NT - 1, -1, -1):
            if c <= 5 and c % 2 == 1:
                S.add("dve", vts(Hst[:], Hst[:], C("flag")), reads=["Hst"], writes=["Hst"])
            set_Hb()
            chunk(c, 1)
            if c % 2 == 0:
                store_H(c // 2, 1)
            load(y2, yp_h[c * 128:(c + 1) * 128, :], [y2k], rkeys=[ypk[c]])
            load(zl, zs_h[c * 128:(c + 1) * 128, :], [zlk], rkeys=zkeys)
            S.add("dve", vtt(y2, y2, yf, ALU.add), reads=[y2k, yfk], writes=[y2k])
            S.add("dve", vtt(yf.rearrange("p (h e) -> p h e", e=64), xs_t.rearrange("p (h e) -> p h e", e=64), bc64(dtot, 0, 16), ALU.mult),
                  reads=[xstk, "dtot", y2k], writes=[yfk])
            S.add("dve", vtt(y2, y2, yf, ALU.add), reads=[y2k, yfk], writes=[y2k])
            S.add("dve", vtt(y2, y2, zl, ALU.mult), reads=[y2k, zlk], writes=[y2k])
            S.add("dve", vtt(yf, y2, y2, ALU.mult), reads=[y2k], writes=[yfk])
            ssg = small[:, 40:42]
            S.add("dve", (lambda o=ssg, i=yf: nc.vector.reduce_sum(out=o, in_=i.rearrange("p (g d) -> p g d", d=512), axis=mybir.AxisListType.X)),
                  reads=[yfk], writes=["ssg"])
            rstd_from_ss(ssg, 512, "ssg", "ssg")
            S.add("dve", vtt(y2.rearrange("p (g d) -> p g d", d=512), y2.rearrange("p (g d) -> p g d", d=512),
                             bass.AP(small, 40, [[64, 128], [1, 2], [0, 512]]), ALU.mult), reads=[y2k, "ssg"], writes=[y2k])
            S.add("dve", vtt(odb, y2, gss, ALU.mult), reads=[y2k, gsk], writes=[odk])
            psb = ps[7][:, :].bitcast(BF16)
            S.add("pe", seq([tr(psb[:, i * 128:(i + 1) * 128], odb[:, i * 128:(i + 1) * 128], identb[:]) for i in range(8)]), reads=[odk], writes=[("ps", 7)])
            S.add("act", act(odT, psb[:, 0:1024].rearrange("p (a t) -> p a t", t=128), AF.Copy), reads=[("ps", 7)], writes=[odTk])
            out_keys.append(store(mixT_h[24 * 128:32 * 128, c * 128:(c + 1) * 128].rearrange("(a p) t -> p a t", p=128), odT, [odTk]))
        S.barrier()

    def phase_op(l):
        pmark('phase_op')
        AR.reset()
        for q in range(4):
            load(bufA[:, q * 8:(q + 1) * 8, :], mixT_h[q * 1024:(q + 1) * 1024, :].rearrange("(j p) t -> p j t", p=128), ["hTall"])
        sts = [AR.alloc([128, 256], F32) for _ in range(4)]
        i = 0
        for cb in range(16):
            s = wload(w_out[l][:, cb * 256:(cb + 1) * 256], 256)
            for tt in range(NT):
                b = bank()
                proj_tm(s, tt, 256, b)
                st, stk = sts[i % 4]
                i += 1
                if i % 2 == 0:
                    S.add("act", act(st, ps[b][:, 0:256], AF.Copy), reads=[("ps", b)], writes=[stk])
                else:
                    S.add("dve", vcp(st, ps[b][:, 0:256]), reads=[("ps", b)], writes=[stk])
                store(ymix[tt * 128:(tt + 1) * 128, cb * 256:(cb + 1) * 256], st, [stk])
        S.barrier()

    def gtg_build(l, v, gp_d, gt, gtk, tmp, tmpk):
        load(gt, bcast_row(modraw[l, v * D:(v + 1) * D]), [gtk])
        load(tmp, bcast_row(bmod_d[l, v * D:(v + 1) * D]), [tmpk])
        S.add("dve", vtt(gt, gt, tmp, ALU.add), reads=[gtk, tmpk], writes=[gtk])
        load(tmp, bcast_row(gp_d[l]), [tmpk])
        S.add("dve", vtt(gt, gt, tmp, ALU.mult), reads=[gtk, tmpk], writes=[gtk])

    def phase_resid(l, v, gp_d, x_src, x_dst, to_h2):
        pmark('phase_resid')
        AR.reset()
        gt, gtk = AR.alloc([128, D], F32)
        xt, xk = AR.alloc([128, D], F32)
        yt, yk = AR.alloc([128, D], F32)
        junk, jk = AR.alloc([128, D], BF16)
        gtg_build(l, v, gp_d, gt, gtk, xt, xk)
        for tt in range(NT):
            rows = slice(tt * 128, (tt + 1) * 128)
            load(yt, ymix[rows, :], [yk])
            load(xt, x_src[rows, :], [xk])
            ss = small[:, tt:tt + 1]; sk = ("ss", tt)
            sumsq(yt, yk, junk, jk, ss, sk)
            rstd_from_ss(ss, D, sk, sk)
            S.add("dve", vstt(yt, yt, ss, gt, ALU.mult, ALU.mult), reads=[yk, sk, gtk], writes=[yk])
            S.add("dve", vtt(xt, xt, yt, ALU.add), reads=[xk, yk], writes=[xk])
            k = store(x_dst[rows, :], xt, [xk])
            if x_dst is y_d:
                out_keys.append(k)
            if to_h2:
                ss2 = small[:, 8 + tt:9 + tt]; sk2 = ("ssb", tt)
                sumsq(xt, xk, junk, jk, ss2, sk2)
                rstd_from_ss(ss2, D, sk2, sk2)
                S.add("dve", vts(yt, xt, ss2), reads=[xk, sk2, yk], writes=[yk])
                norm_to_hT(yt, yk, tt, gmod2T, modT[:, 96:128])
        S.barrier()

    def phase_ffn_up(l):
        pmark('phase_ffn_up')
        AR.reset()
        ust = [AR.alloc([128, 2, T], BF16) for _ in range(2)]
        rl = [AR.alloc([128, 512], F32) for _ in range(2)]
        i = 0
        for fb in range(64):
            s = wload(w_up[l][:, fb * 256:(fb + 1) * 256], 256)
            u_, ukk = ust[fb % 2]
            for hh in range(2):
                b0 = bank(); b1 = bank()
                proj_fm(s, hh * 128, b0, b1)
                for th, b in ((0, b0), (1, b1)):
                    r_, rk_ = rl[i % 2]
                    i += 1
                    S.add("act", act(r_, ps[b][:, :], AF.Relu), reads=[("ps", b)], writes=[rk_])
                    S.add("dve", vtt(u_[:, hh, th * 512:(th + 1) * 512], ps[b][:, :], r_, ALU.mult), reads=[("ps", b), rk_], writes=[ukk])
            store(uT_h[fb * 256:(fb + 1) * 256, :].rearrange("(c p) t -> p c t", p=128), u_, [ukk])
        S.barrier()

    def phase_ffn_down(l):
        pmark('phase_ffn_down')
        AR.reset()
        usl = [AR.alloc([128, 16, T], BF16) for _ in range(2)]
        sts = [A("dst%d" % i, [128, 512], F32) for i in range(2)] if "dst" not in _once else _once["dst"]
        _once["dst"] = sts
        ui = 0
        for cg in range(8):
            for fg in range(8):
                s = cnt["ring"] % NRING
                cnt["ring"] += 1
                wv = ring[s][:].rearrange("p j n -> p (j n)").rearrange("p (a b) -> p a b", b=512)
                S.add("pool", dma(wv, w_down[l][fg * 2048:(fg + 1) * 2048, cg * 512:(cg + 1) * 512].rearrange("(j p) n -> p j n", p=128), q="pool"),
                      writes=[("ring", s)], chan=ch_ring[s])
                u_, ukk = usl[ui % 2]
                ui += 1
                load(u_, uT_h[fg * 2048:(fg + 1) * 2048, :].rearrange("(j p) t -> p j t", p=128), [ukk])
                for tt in range(NT):
                    S.add("pe", seq([mm(ps[tt][:, :], u_[:, jj, tt * 128:(tt + 1) * 128], wv[:, jj, :],
                                        fg == 0 and jj == 0, fg == 7 and jj == 15) for jj in range(16)]),
                          reads=[ukk, ("ring", s)], writes=[("ps", tt)])
            for tt in range(NT):
                st = sts[tt % 2]
                stk = ("dst", tt % 2)
                if tt % 2 == 0:
                    S.add("act", act(st[:], ps[tt][:, :], AF.Copy), reads=[("ps", tt)], writes=[stk])
                else:
                    S.add("dve", vcp(st[:], ps[tt][:, :]), reads=[("ps", tt)], writes=[stk])
                store(ymix[tt * 128:(tt + 1) * 128, cg * 512:(cg + 1) * 512], st[:], [stk])
        S.barrier()

    _once = {}

    def done(name):
        return stop_after == name

    for l in range(2):
        if "skipmod" not in dbg:
            phase_mod(l)
        if l == 0 and done("mod"):
            dump("modT", modT[:], [], [128, 192])
            return finish()
        phase_n1(l, x_in if l == 0 else xl1)
        if l == 0 and done("n1"):
            for j in (0, 31):
                dump("hT%d" % j, bufA[:, j, :], [], [128, T])
            return finish()
        S.add("dve", vset(small[:, 20:21], 0.0), writes=["hTall"])
        S.barrier()
        for nm, ph in (("mixA", phase_mixA), ("mixB", phase_mixB), ("mixC", phase_mixC), ("mixD", phase_mixD)):
            if ("only_" in " ".join(dbg)) and ("only_" + nm) not in dbg:
                continue
            ph(l)
            if l == 0 and done(nm):
                dump("small", small[:], [], [128, 64])
                if "mixT" in dbg:
                    d_ = dout("dbg_mixT", [D, T], BF16)
                    dbg_out["mixT"] = d_
                    out_keys.append(store(d_, mixT_h, []))
                return finish()
        phase_op(l)
        if l == 0 and done("op"):
            return finish()
        phase_resid(l, 2, gpost_d, x_in if l == 0 else xl1, xmid, True)
        if l == 0 and done("n2"):
            return finish()
        phase_ffn_up(l)
        phase_ffn_down(l)
        phase_resid(l, 5, gpostf_d, xmid, xl1 if l == 0 else y_d, False)
        if l == 0 and done("l0"):
            for nm_, t_ in (("xl1", xl1), ("xmid", xmid)):
                if nm_ in dbg:
                    d_ = dout("dbg_" + nm_, [T, D])
                    for q in range(4):
                        out_keys.append(store(d_[q * 256:(q + 1) * 256, :], t_[q * 256:(q + 1) * 256, :], []))
            return finish()
    S.add("sp", None, reads=out_keys)
    return finish()


def _rope_tables(identity):
    tabs = np.zeros((4, 128, T), np.float32)
    t = np.arange(T)
    row = (t // 64).astype(np.float64)
    col = (t % 64).astype(np.float64)

    def fill(cosT, sinT, R, base, d):
        da = d // 2
        inv = 1.0 / (10000.0 ** (np.arange(0, da, 2, dtype=np.float64) / da))
        hf = da // 2
        for part, pos in ((0, row), (1, col)):
            o = base + part * da
            ang = pos[None, :] * inv[:, None]
            cs, sn = np.cos(ang), np.sin(ang)
            cosT[o:o + hf] = cs; cosT[o + hf:o + da] = cs
            sinT[o:o + hf] = sn; sinT[o + hf:o + da] = sn
            for i in range(hf):
                R[o + hf + i, o + i] = -1.0
                R[o + i, o + hf + i] = 1.0
    RA = np.zeros((128, 128), np.float32); RBC = np.zeros((128, 128), np.float32)
    fill(tabs[0], tabs[1], RA, 0, 64); fill(tabs[0], tabs[1], RA, 64, 64)
    fill(tabs[2], tabs[3], RBC, 0, 128)
    if identity:
        tabs[0] = 1.0; tabs[1] = 0.0; tabs[2] = 1.0; tabs[3] = 0.0
    return tabs, RA, RBC


def _prep_inputs(inp):
    f = np.float32
    lay = CL
    shared = {k: np.ascontiguousarray(inp[k], dtype=f) for k in ("w_mod", "w_in", "w_out", "w_up", "w_down", "b_mod", "g_post_mix", "g_post_ffn")}
    gssm = np.ascontiguousarray(np.broadcast_to(np.asarray(inp["g_ssm_norm"], f)[:, None, :], (2, 128, 1024)))
    maps = []
    ii = np.arange(128)
    for core in range(8):
        sample = core < 4
        cst = np.zeros((128, NCST), f)

        def put(name, arr):
            o, n = lay[name]
            cst[:, o:o + n] = arr
        put("ident", np.eye(128, dtype=f))
        put("triU", (ii[:, None] <= ii[None, :]).astype(f)); put("triL", (ii[:, None] >= ii[None, :]).astype(f))
        put("sL", (ii[:, None] > ii[None, :]).astype(f)); put("sU", (ii[:, None] < ii[None, :]).astype(f))
        put("ones", np.ones((128, 128), f))
        tabs, RA, RBC = _rope_tables(identity=not sample)
        put("RA", RA); put("RBC", RBC)
        put("flag", 1.0 if sample else 0.0)
        mab = np.zeros((10, 4), f)
        if not sample:
            mab[:] = NEG
            for kt in range(2, 10):
                seq_k = (kt - 2) // 2
                mab[kt, seq_k] = 0.0
        put("maskAB", np.broadcast_to(mab.reshape(1, 40), (128, 40)))
        put("maskC", 0.0 if sample else NEG)
        for l in range(2):
            put("bmodT%d" % l, np.asarray(inp["b_mod"], f)[l].reshape(192, 128).T)
            for g, nm in (("gpre", "g_pre_mix"), ("gpost", "g_post_mix"), ("gpref", "g_pre_ffn"), ("gpostf", "g_post_ffn")):
                put("%sT%d" % (g, l), np.asarray(inp[nm], f)[l].reshape(32, 128).T)
            put("gsubT%d" % l, np.asarray(inp["g_subln"], f)[l].reshape(128, 1))
            put("gqT%d" % l, np.asarray(inp["g_qnorm"], f)[l].reshape(128, 1))
            put("gkT%d" % l, np.asarray(inp["g_knorm"], f)[l].reshape(128, 1))
            put("gk_bc%d" % l, np.broadcast_to(np.asarray(inp["g_knorm"], f)[l][None, :], (128, 128)))
            put("sink_bc%d" % l, np.broadcast_to(np.asarray(inp["sink"], f)[l][None, :], (128, 8)))
            for a, nm in (("lq1", "lam_q1"), ("lk1", "lam_k1"), ("lq2", "lam_q2"), ("lk2", "lam_k2")):
                put("%s_%d" % (a, l), np.broadcast_to(np.asarray(inp[nm], f)[l][None, :], (128, 64)))
            cw = np.asarray(inp["conv_w"], f)[l]
            put("convw%d" % l, cw.reshape(5, 12, 128).transpose(2, 1, 0).reshape(128, 60))
            put("convb%d" % l, np.asarray(inp["conv_b"], f)[l].reshape(12, 128).T)
            put("dtb%d" % l, np.broadcast_to(np.asarray(inp["dt_bias"], f)[l].reshape(1, 32), (128, 32)))
            put("alog%d" % l, np.broadcast_to(np.asarray(inp["a_log"], f)[l].reshape(1, 32), (128, 32)))
            put("dsk%d" % l, np.broadcast_to(np.asarray(inp["d_skip"], f)[l].reshape(1, 32), (128, 32)))
        cm = np.zeros((128, 8, 2, 128), f)
        for n in range(8):
            if sample:
                cm[:, n, 0, :] = (ii[None, :] <= ii[:, None])
                cm[:, n, 1, :] = (ii[:, None] <= ii[None, :])
            else:
                cm[:, n, 0, :] = 1.0 if n % 2 == 1 else 0.0
                cm[:, n, 1, :] = 1.0 if n % 2 == 0 else 0.0
        m = dict(shared)
        m["cst"] = cst; m["tabs"] = tabs; m["cmask"] = cm.reshape(128, 2048); m["gssm"] = gssm
        if sample:
            b = core
            m["x_in"] = np.ascontiguousarray(inp["x_sample"][b], f)
            m["cT"] = np.ascontiguousarray(np.asarray(inp["c"], f)[b].reshape(32, 128).T)
            m["cak"] = np.ascontiguousarray(np.asarray(inp["cache_a_k"], f)[b].reshape(2, 256, 1024))
            m["cav"] = np.ascontiguousarray(np.asarray(inp["cache_a_v"], f)[b].reshape(2, 256, 1024))
            m["cbk"] = np.ascontiguousarray(np.asarray(inp["cache_b_k"], f)[b].reshape(2, 256, 256))
            m["cbv"] = np.ascontiguousarray(np.asarray(inp["cache_b_v"], f)[b].reshape(2, 256, 256))
            m["cck"] = np.ascontiguousarray(np.asarray(inp["cache_c_k"], f)[b].reshape(2, 256, 256))
            m["ccv"] = np.ascontiguousarray(np.asarray(inp["cache_c_v"], f)[b].reshape(2, 256, 256))
            m["h0"] = np.ascontiguousarray(np.asarray(inp["state_ssm"], f)[b].reshape(2, 2, 1024, 128))
        else:
            s0 = (core - 4) * 4
            m["x_in"] = np.ascontiguousarray(np.asarray(inp["x_prompt"], f)[s0:s0 + 4].reshape(T, D))
            m["cT"] = np.ascontiguousarray(np.asarray(inp["c_ctx"], f).reshape(32, 128).T)
            for k, w in (("cak", 1024), ("cav", 1024), ("cbk", 256), ("cbv", 256), ("cck", 256), ("ccv", 256)):
                m[k] = np.zeros((2, 256, w), f)
            m["h0"] = np.zeros((2, 2, 1024, 128), f)
        maps.append(m)
    return maps


_CACHE = {}


def kernel(**inputs):
    maps = _prep_inputs(inputs)
    if "nc" not in _CACHE:
        _CACHE["nc"] = build()[0]
    res = run_bass_kernel_spmd(_CACHE["nc"], maps, core_ids=list(range(8)))
    r = res.results
    f = np.float32
    y_sample = np.stack([r[c]["y"] for c in range(4)], 0).astype(f)
    y_prompt = np.concatenate([r[c]["y"].reshape(4, 256, D) for c in range(4, 8)], 0).astype(f)

    def gather(name, shp):
        parts = []
        for c in range(4, 8):
            a = r[c][name]
            a = a.reshape(2, 4, 256, -1).transpose(1, 0, 2, 3)
            parts.append(a)
        return np.ascontiguousarray(np.concatenate(parts, 0).reshape(shp)).astype(f)
    nak_ = gather("nak", (16, 2, 256, 8, 128)); nav_ = gather("nav", (16, 2, 256, 8, 128))
    nbk_ = gather("nbk", (16, 2, 256, 2, 128)); nbv_ = gather("nbv", (16, 2, 256, 2, 128))
    nck_ = gather("nck", (16, 2, 256, 2, 128)); ncv_ = gather("ncv", (16, 2, 256, 2, 128))
    ss = []
    for c in range(4, 8):
        a = r[c]["nssm"]
        ss.append(a.transpose(1, 0, 2, 3, 4))
    nssm_ = np.ascontiguousarray(np.concatenate(ss, 0).reshape(16, 2, 2, 16, 64, 128)).astype(f)
    return (y_prompt, y_sample, nak_, nav_, nbk_, nbv_, nck_, ncv_, nssm_)
```
